# Optimizing a Trainium2 kernel written in Bass

```python
import math
import jax, jax.numpy as jnp
from jax import lax
import numpy as np

D_MODEL = 1024
BATCH = 32
SEQ = 2048
DEPTH = 4

N_MIXERS = 2
SB_HEADS = 16
SB_HEAD_DIM = D_MODEL // SB_HEADS
Q_BLOCK = 128
GMLP_WIDTH = 2 * D_MODEL
GMLP_GROUPS = 8
GMLP_CHUNK = 128
D_FF = ((8 * D_MODEL // 3 + 127) // 128) * 128
CONV_WIDTH = 3
LN_EPS = 1e-5
DEEPNORM_ALPHA = (2 * DEPTH) ** 0.25
DEEPNORM_BETA = (8 * DEPTH) ** -0.25
N_ATTN_LAYERS = (DEPTH + 1) // 2
N_GMLP_LAYERS = DEPTH // 2

kernel_name = "sb_attn_gmlp_convffn_deepnorm_hybrid"


def layer_norm(h, g, b):
    hf = h.astype(jnp.float32)
    mu = jnp.mean(hf, axis=-1, keepdims=True)
    var = jnp.mean(jnp.square(hf - mu), axis=-1, keepdims=True)
    y = (hf - mu) * lax.rsqrt(var + LN_EPS)
    return (y * g.astype(jnp.float32) + b.astype(jnp.float32)).astype(h.dtype)


def stick_breaking_attention(h, w_in, w_out):
    B, S, _ = h.shape
    qkv = (h @ w_in).reshape(B, S, 3, SB_HEADS, SB_HEAD_DIM)
    q = jnp.transpose(qkv[:, :, 0], (0, 2, 1, 3))
    k = jnp.transpose(qkv[:, :, 1], (0, 2, 1, 3))
    v = jnp.transpose(qkv[:, :, 2], (0, 2, 1, 3))
    scale = SB_HEAD_DIM ** -0.5
    outs = []
    for blk in range(S // Q_BLOCK):
        q0 = blk * Q_BLOCK
        k_end = q0 + Q_BLOCK
        qb = q[:, :, q0:k_end]
        kb = k[:, :, :k_end]
        vb = v[:, :, :k_end]
        z = jnp.einsum('bhqd,bhkd->bhqk', qb, kb).astype(jnp.float32) * scale
        t_idx = q0 + jnp.arange(Q_BLOCK)[:, None]
        s_idx = jnp.arange(k_end)[None, :]
        causal = s_idx < t_idx
        log_beta = jax.nn.log_sigmoid(z)
        log_one_minus = jnp.where(causal, jax.nn.log_sigmoid(-z), 0.0)
        suffix = lax.cumsum(log_one_minus, axis=3, reverse=True) - log_one_minus
        a = jnp.where(causal, jnp.exp(log_beta + suffix), 0.0)
        outs.append(jnp.einsum('bhqk,bhkd->bhqd', a.astype(vb.dtype), vb))
    o = jnp.concatenate(outs, axis=2)
    o = jnp.transpose(o, (0, 2, 1, 3)).reshape(B, S, D_MODEL)
    return o @ w_out


def chunked_spatial_gating(h, w_in, ln_g, ln_b, w_s, b_s, w_out):
    B, S, _ = h.shape
    zz = jax.nn.gelu(h @ w_in)
    u, v = zz[..., :GMLP_WIDTH], zz[..., GMLP_WIDTH:]
    v = layer_norm(v, ln_g, ln_b)
    v = v.reshape(B, S // GMLP_CHUNK, GMLP_CHUNK, GMLP_GROUPS, GMLP_WIDTH // GMLP_GROUPS)
    tri = jnp.tril(jnp.ones((GMLP_CHUNK, GMLP_CHUNK), dtype=bool))
    w_causal = jnp.where(tri[None], w_s, 0.0).astype(v.dtype)
    s = jnp.einsum('gts,bnsgc->bntgc', w_causal, v)
    s = s + jnp.transpose(b_s, (1, 0))[None, None, :, :, None].astype(s.dtype)
    s = s.reshape(B, S, GMLP_WIDTH)
    return (u * s) @ w_out


def causal_depthwise_conv(a, w, b):
    S = a.shape[1]
    pad = CONV_WIDTH - 1
    ap = jnp.pad(a, ((0, 0), (pad, 0), (0, 0)))
    y = b
    for tap in range(CONV_WIDTH):
        y = y + w[tap] * ap[:, tap:tap + S]
    return y


def conv_gated_ffn(h, w_up, conv_w, conv_b, w_down):
    a = h @ w_up
    a = causal_depthwise_conv(a, conv_w, conv_b)
    gate, val = a[..., :D_FF], a[..., D_FF:]
    return (jax.nn.silu(gate) * val) @ w_down


def setup_inputs(seed: int = 0) -> dict:
    key = jax.random.key(seed)
    ks = jax.random.split(key, 20)
    f32 = jnp.float32
    D, E, G, C, F = D_MODEL, GMLP_WIDTH, GMLP_GROUPS, GMLP_CHUNK, D_FF
    nrm = lambda k, shape, std: jax.random.normal(k, shape, f32) * std
    x = nrm(ks[0], (BATCH, SEQ, D), 1.0)
    attn_qk = nrm(ks[1], (N_ATTN_LAYERS, D, 2 * D), D ** -0.5)
    attn_v = nrm(ks[2], (N_ATTN_LAYERS, D, D), D ** -0.5 * DEEPNORM_BETA)
    attn_w_in = jnp.concatenate([attn_qk, attn_v], axis=-1)
    attn_w_out = nrm(ks[3], (N_ATTN_LAYERS, D, D), D ** -0.5 * DEEPNORM_BETA)
    gmlp_w_in = nrm(ks[4], (N_GMLP_LAYERS, D, 2 * E), D ** -0.5)
    gmlp_ln_g = 1.0 + nrm(ks[5], (N_GMLP_LAYERS, E), 0.02)
    gmlp_ln_b = nrm(ks[6], (N_GMLP_LAYERS, E), 0.02)
    gmlp_w_s = nrm(ks[7], (N_GMLP_LAYERS, G, C, C), C ** -0.5)
    gmlp_b_s = 1.0 + nrm(ks[8], (N_GMLP_LAYERS, G, C), 0.02)
    gmlp_w_out = nrm(ks[9], (N_GMLP_LAYERS, E, D), E ** -0.5 * DEEPNORM_BETA)
    ffn_w_up = nrm(ks[10], (DEPTH, D, 2 * F), D ** -0.5)
    ffn_conv_w = nrm(ks[11], (DEPTH, CONV_WIDTH, 2 * F), CONV_WIDTH ** -0.5)
    ffn_conv_b = nrm(ks[12], (DEPTH, 2 * F), 0.01)
    ffn_w_down = nrm(ks[13], (DEPTH, F, D), F ** -0.5 * DEEPNORM_BETA)
    ln_mix_g = 1.0 + nrm(ks[14], (DEPTH, D), 0.02)
    ln_mix_b = nrm(ks[15], (DEPTH, D), 0.02)
    ln_ffn_g = 1.0 + nrm(ks[16], (DEPTH, D), 0.02)
    ln_ffn_b = nrm(ks[17], (DEPTH, D), 0.02)
    return {"x": x, "attn_w_in": attn_w_in, "attn_w_out": attn_w_out,
            "gmlp_w_in": gmlp_w_in, "gmlp_ln_g": gmlp_ln_g, "gmlp_ln_b": gmlp_ln_b,
            "gmlp_w_s": gmlp_w_s, "gmlp_b_s": gmlp_b_s, "gmlp_w_out": gmlp_w_out,
            "ffn_w_up": ffn_w_up, "ffn_conv_w": ffn_conv_w, "ffn_conv_b": ffn_conv_b,
            "ffn_w_down": ffn_w_down, "ln_mix_g": ln_mix_g, "ln_mix_b": ln_mix_b,
            "ln_ffn_g": ln_ffn_g, "ln_ffn_b": ln_ffn_b}


def reference(x, attn_w_in, attn_w_out, gmlp_w_in, gmlp_ln_g, gmlp_ln_b, gmlp_w_s, gmlp_b_s,
              gmlp_w_out, ffn_w_up, ffn_conv_w, ffn_conv_b, ffn_w_down,
              ln_mix_g, ln_mix_b, ln_ffn_g, ln_ffn_b):
    h = x
    for i in range(DEPTH):
        j = i // N_MIXERS
        if i % N_MIXERS == 0:
            m = stick_breaking_attention(h, attn_w_in[j], attn_w_out[j])
        else:
            m = chunked_spatial_gating(h, gmlp_w_in[j], gmlp_ln_g[j], gmlp_ln_b[j],
                                       gmlp_w_s[j], gmlp_b_s[j], gmlp_w_out[j])
        h = layer_norm(DEEPNORM_ALPHA * h + m, ln_mix_g[i], ln_mix_b[i])
        f = conv_gated_ffn(h, ffn_w_up[i], ffn_conv_w[i], ffn_conv_b[i], ffn_w_down[i])
        h = layer_norm(DEEPNORM_ALPHA * h + f, ln_ffn_g[i], ln_ffn_b[i])
    return h
```

```python
import contextlib
import numpy as np
import concourse.bass as bass
import concourse.mybir as mybir
from concourse.bass_utils import run_bass_kernel_spmd

F32 = mybir.dt.float32
BF16 = mybir.dt.bfloat16
AF = mybir.ActivationFunctionType
ALU = mybir.AluOpType
AX = mybir.AxisListType

D = 1024
NCH = 8
HEADS = 16
E2 = 2048
GG = 8
FF = 2816
NFF = 22
DEPTH_FULL = 4
ALPHA = (2 * DEPTH_FULL) ** 0.25
INV_ALPHA = 1.0 / ALPHA
EPS_RES = 1e-5 / (ALPHA * ALPHA)
EPS_G = 1e-5
ROT = 20000
NFILL = 2


class _Op:
    __slots__ = ("id", "eng", "fn", "deps", "dma", "ndma", "sig")


class Sched:
    ENGS = ("pe", "act", "dve", "pool", "sp")

    def __init__(self):
        self.ops = []
        self.lastw = {}
        self.readers = {}
        self.barrier_deps = set()
        self.since = {}
        self.dma_since = []

    def add(self, eng, fn, r=(), w=(), dma=None, ndma=1, nobar=False):
        o = _Op()
        o.id = len(self.ops); o.eng = eng; o.fn = fn; o.dma = dma; o.ndma = ndma; o.sig = None
        deps = set(self.barrier_deps)
        for k in r:
            if k in self.lastw:
                deps.add(self.lastw[k])
        for k in w:
            if k in self.lastw:
                deps.add(self.lastw[k])
            for rid in self.readers.get(k, {}).values():
                deps.add(rid)
        keep = set()
        for d in deps:
            p = self.ops[d]
            if p.dma is None and dma is None and p.eng == eng:
                if eng == "pe":
                    continue
                raw = any(self.lastw.get(k) == d for k in r) or any(self.lastw.get(k) == d for k in w)
                if not raw and d not in self.barrier_deps:
                    continue
            keep.add(d)
        o.deps = keep
        for k in r:
            rk = self.readers.setdefault(k, {})
            rk[eng if dma is None else ("dma", o.id)] = o.id
        for k in w:
            self.lastw[k] = o.id
            self.readers[k] = {}
        self.ops.append(o)
        if dma is None:
            self.since[eng] = o.id
        elif not nobar:
            self.dma_since.append(o.id)
        return o.id

    def barrier(self):
        self.barrier_deps = set(self.since.values()) | set(self.dma_since)
        self.dma_since = []

    def emit(self, nc, st):
        ops = self.ops
        has_dep = [False] * len(ops)
        for o in ops:
            for d in o.deps:
                has_dep[d] = True
        cnt = {e: 0 for e in self.ENGS}
        dcnt = {}
        for o in ops:
            if o.dma is None:
                if has_dep[o.id]:
                    c = cnt[o.eng]; cnt[o.eng] = c + 1
                    o.sig = (("E", o.eng, c // ROT), c % ROT + 1)
            else:
                c = dcnt.get(o.dma, 0) + o.ndma; dcnt[o.dma] = c
                o.sig = (("D", o.dma), 16 * c)
        sems = {}
        for e in self.ENGS:
            for k in range((cnt[e] + ROT - 1) // ROT):
                sems[("E", e, k)] = st.enter_context(nc.semaphore(f"s_{e}{k}"))
        for i, k in enumerate(sorted(dcnt, key=str)):
            sems[("D", k)] = st.enter_context(nc.semaphore(f"d{i}"))
        self.nsems = len(sems)
        byeng = {e: [] for e in self.ENGS}
        for o in ops:
            byeng[o.eng].append(o)

        def run(ename, eng):
            waited = {}
            for o in byeng[ename]:
                need = {}
                for d in o.deps:
                    s, v = ops[d].sig
                    if need.get(s, 0) < v:
                        need[s] = v
                for s, v in need.items():
                    if waited.get(s, 0) < v:
                        eng.wait_ge(sems[s], v)
                        waited[s] = v
                if o.fn is None:
                    continue
                ins = o.fn(eng)
                if o.dma is not None:
                    if not isinstance(ins, (list, tuple)):
                        ins = [ins]
                    assert len(ins) == o.ndma, (len(ins), o.ndma)
                    for x in ins:
                        x.then_inc(sems[("D", o.dma)], 16)
                elif has_dep[o.id]:
                    ins.then_inc(sems[o.sig[0]], 1)

        block = st.enter_context(nc.Block())

        @block.tensor
        def _(e):
            run("pe", e)

        @block.scalar
        def _(e):
            run("act", e)

        @block.vector
        def _(e):
            run("dve", e)

        @block.gpsimd
        def _(e):
            run("pool", e)

        @block.sync
        def _(e):
            run("sp", e)


def build(nseq, S, depth, ffn_t=1024, dbg=()):
    nc = bass.Bass("TRN2", target_bir_lowering=False)
    NA = (depth + 1) // 2
    NG = depth // 2
    NTB = S // 128
    TT = 512
    NT = S // TT
    FT = min(ffn_t, S)
    NFT = S // FT

    def din(name, shape, dt=F32):
        return nc.dram_tensor(name, list(shape), dt, kind="ExternalInput").ap()

    x_d = din("x", [nseq, S, D])
    y_d = nc.dram_tensor("y", [nseq, S, D], F32, kind="ExternalOutput").ap()
    w_ain = din("attn_w_in", [NA, D, 3 * D])
    w_aout = din("attn_w_out", [NA, D, D])
    w_gin = din("gmlp_w_in", [max(NG, 1), D, 2 * E2])
    w_gout = din("gmlp_w_out", [max(NG, 1), E2, D])
    w_up = din("ffn_w_up", [depth, D, 2 * FF])
    w_dn = din("ffn_w_down", [depth, FF, D])
    ws_d = din("gmlp_w_s", [max(NG, 1), GG, 128, 128])
    bs_d = din("bs", [max(NG, 1), 1, GG * 128])
    gb_d = din("gb", [max(NG, 1), 2, 128, E2])
    lnp_d = din("lnp", [128, depth * 32])
    conv_d = din("convp", [128, depth * 4 * 44])
    cst_d = din("consts", [128, 6 * 128])

    def dscr(name, shape):
        return nc.dram_tensor(name, list(shape), BF16, kind="Internal").ap()

    b_ain = dscr("b_ain", [NA, D, 3 * D])
    b_aout = dscr("b_aout", [NA, D, D])
    b_gin = dscr("b_gin", [max(NG, 1), D, 2 * E2])
    b_gout = dscr("b_gout", [max(NG, 1), E2, D])
    b_up = dscr("b_up", [depth, D, 2 * FF])
    b_dn = dscr("b_dn", [depth, FF, D])

    sc = Sched()
    st = contextlib.ExitStack()
    with st:
        E = st.enter_context
        h32 = E(nc.sbuf_tensor("h32", [128, NCH, S], F32))
        AB_N = 43 * 1024
        AF_N = 11 * 1024
        arena_b = E(nc.sbuf_tensor("arena_b", [128, AB_N], BF16))
        arena_f = E(nc.sbuf_tensor("arena_f", [128, AF_N], F32))
        cst_f = E(nc.sbuf_tensor("cst_f", [128, 6 * 128], F32))
        cst_b = E(nc.sbuf_tensor("cst_b", [128, 6 * 128], BF16))
        lnp = E(nc.sbuf_tensor("lnp_sb", [128, depth * 32], F32))
        convp = E(nc.sbuf_tensor("convp_sb", [128, depth * 176], F32))
        wct = E(nc.sbuf_tensor("wct", [128, max(NG, 1) * GG * 128], BF16))
        ps = E(nc.psum_tensor("ps", [128, 4096], F32))

        ident = cst_f[:, 0:128]
        negtri = cst_b[:, 128:256]
        negones = cst_b[:, 256:384]
        mask_su = cst_b[:, 384:512]
        mask_ui_f = cst_f[:, 512:640]
        onesd = cst_b[:, 640:768]

        class Arena:
            def __init__(self, t, n, esz):
                self.t = t; self.n = n; self.off = 0; self.esz = esz

            def reset(self):
                self.off = 0

            def alloc(self, *dims):
                n = int(np.prod(dims))
                a = self.off
                self.off = a + ((n + 31) // 32) * 32
                assert self.off <= self.n, ("arena overflow", self.off, self.n)
                v = self.t[:, a:a + n]
                self.last_flat = v
                if len(dims) == 2:
                    v = v.rearrange("p (a b) -> p a b", a=dims[0])
                elif len(dims) == 3:
                    v = v.rearrange("p (a b c) -> p a b c", a=dims[0], b=dims[1])
                return v

        ab = Arena(arena_b, AB_N, 2)
        af = Arena(arena_f, AF_N, 4)

        def bank(i, n=1):
            return ps[:, i * 512:(i + n) * 512]

        def PK(i, n=1):
            return [("ps", i + k) for k in range(n)]

        sc.add("sp", lambda e: e.dma_start(out=cst_f[:, :], in_=cst_d[:, :]), w=["cst_f"], dma="cst")
        sc.add("sp", lambda e: e.dma_start(out=lnp[:, :], in_=lnp_d[:, :]), w=["lnp"], dma="lnp")
        sc.add("sp", lambda e: e.dma_start(out=convp[:, :], in_=conv_d[:, :]), w=["convp"], dma="convp")
        sc.add("dve", lambda e: e.tensor_copy(out=cst_b[:, :], in_=cst_f[:, :]), r=["cst_f"], w=["cst_b"])

        cast_q = []
        castkeys = {}
        cur_ph = [0]

        def cast_all(dst, src, key, rows, step=256):
            keys = []
            for r0 in range(0, rows, step):
                r1 = min(rows, r0 + step)
                k = key + (r0,)
                keys.append(k)

                def emit(d=dst, s_=src, a=r0, b=r1, k=k, key=key):
                    sc.add("pool", lambda e: e.dma_start(out=d[a:b, :], in_=s_[a:b, :]),
                           w=[k], dma="wc_%s_%d" % key, nobar=True)
                cast_q.append((cur_ph[0], emit))
            castkeys[key] = keys

        def drain(n):
            for _ in range(n):
                if cast_q:
                    cast_q.pop(0)[1]()

        def flush_until(ph):
            while cast_q and cast_q[0][0] <= ph:
                cast_q.pop(0)[1]()

        def cast_phase(ph):
            if ph >= 2 * depth:
                return
            cur_ph[0] = ph
            l = ph // 2; j = l // 2
            if ph % 2 == 0:
                if l % 2 == 0:
                    cast_all(b_ain[j], w_ain[j], ("c_ain", j), D)
                    cast_all(b_aout[j], w_aout[j], ("c_aout", j), D)
                else:
                    cast_all(b_gin[j], w_gin[j], ("c_gin", j), D)
                    cast_all(b_gout[j], w_gout[j], ("c_gout", j), E2)
            else:
                cast_all(b_up[l], w_up[l], ("c_up", l), D)
                cast_all(b_dn[l], w_dn[l], ("c_dn", l), FF)

        cast_phase(0)
        cast_phase(1)
        flush_until(1)

        ab.reset(); af.reset()
        if NG > 0:
            wst = arena_f[:, AF_N - 256:AF_N].rearrange("p (a b) -> p a b", a=2)
            for j in range(NG):
                for g in range(GG):
                    b = (j * GG + g) % 2
                    sc.add("sp", lambda e, j=j, g=g, b=b: e.dma_start(out=wst[:, b, :], in_=ws_d[j, g, :, :]),
                           w=[("wst", b)], dma=f"wst{b}")
                    sc.add("pe", lambda e, b=b: e.transpose(out=bank(b)[:, 0:128], in_=wst[:, b, :], identity=ident),
                           r=[("wst", b), "cst_f"], w=PK(b))
                    sc.add("dve", lambda e, j=j, g=g, b=b: e.tensor_tensor(
                        out=wct[:, (j * GG + g) * 128:(j * GG + g + 1) * 128], in0=bank(b)[:, 0:128],
                        in1=mask_ui_f, op=ALU.mult), r=PK(b) + ["cst_f"], w=["wct"])

        def hkey(c, t0, t1):
            return [("h", c, k) for k in range(t0 // 512, (t1 + 511) // 512)]

        def hkeys_all(t0, t1):
            out = []
            for c in range(NCH):
                out += hkey(c, t0, t1)
            return out

        def layer_norm(li, which, t0, xb, xsq, kxb, stats):
            mean_sb, var_sb, rstd_sb = stats
            W = 512
            g_off = li * 32 + which * 16
            for hf in range(2):
                cs = slice(4 * hf, 4 * hf + 4)
                hk = []
                for c in range(4 * hf, 4 * hf + 4):
                    hk += hkey(c, t0, t0 + W)
                sc.add("dve", lambda e, cs=cs: e.tensor_copy(out=xb[:, cs, :], in_=h32[:, cs, t0:t0 + W]),
                       r=hk, w=[kxb[0][hf]])
                sc.add("act", lambda e, cs=cs: e.activation(out=xsq[:, cs, :], in_=h32[:, cs, t0:t0 + W],
                                                            func=AF.Square), r=hk, w=[kxb[1][hf]])
            for hf in range(2):
                def mm_stats(e, hf=hf, dst=bank(6), src=xb):
                    for c in range(4 * hf, 4 * hf + 4):
                        i = e.matmul(dst, lhsT=onesd, rhs=src[:, c, :], start=(c == 0), stop=(c == NCH - 1))
                    return i
                sc.add("pe", mm_stats, r=[kxb[0][hf], "cst_b"], w=PK(6))
            for hf in range(2):
                def mm_stats2(e, hf=hf, dst=bank(7), src=xsq):
                    for c in range(4 * hf, 4 * hf + 4):
                        i = e.matmul(dst, lhsT=onesd, rhs=src[:, c, :], start=(c == 0), stop=(c == NCH - 1))
                    return i
                sc.add("pe", mm_stats2, r=[kxb[1][hf], "cst_b"], w=PK(7))
            sc.add("act", lambda e: e.activation(out=mean_sb, in_=bank(6), func=AF.Copy), r=PK(6), w=["ln_mean"])
            sc.add("dve", lambda e: e.tensor_tensor(out=var_sb, in0=mean_sb, in1=mean_sb, op=ALU.mult),
                   r=["ln_mean"], w=["ln_var"])
            sc.add("dve", lambda e: e.scalar_tensor_tensor(out=var_sb, in0=bank(7), scalar=EPS_RES, in1=var_sb,
                                                           op0=ALU.add, op1=ALU.subtract),
                   r=PK(7) + ["ln_var"], w=["ln_var"])
            sc.add("act", lambda e: e.activation(out=var_sb, in_=var_sb, func=AF.Ln), r=["ln_var"], w=["ln_var"])
            sc.add("act", lambda e: e.activation(out=rstd_sb, in_=var_sb, func=AF.Exp, scale=-0.5),
                   r=["ln_var"], w=["ln_rstd"])
            order = [7, 0, 1, 2, 3, 4, 5, 6]
            for c in order:
                eng = "pool" if c >= 7 else "dve"
                hc = h32[:, c, t0:t0 + W]
                kc = hkey(c, t0, t0 + W)
                sc.add(eng, lambda e, hc=hc: e.tensor_tensor(out=hc, in0=hc, in1=mean_sb, op=ALU.subtract),
                       r=kc + ["ln_mean"], w=kc)
                sc.add(eng, lambda e, hc=hc: e.tensor_tensor(out=hc, in0=hc, in1=rstd_sb, op=ALU.mult),
                       r=kc + ["ln_rstd"], w=kc)
            for c in order:
                sc.add("act", lambda e, c=c: e.activation(
                    out=h32[:, c, t0:t0 + W], in_=h32[:, c, t0:t0 + W], func=AF.Identity,
                    scale=lnp[:, g_off + c:g_off + c + 1], bias=lnp[:, g_off + 8 + c:g_off + 9 + c]),
                    r=hkey(c, t0, t0 + W) + ["lnp"], w=hkey(c, t0, t0 + W))

        def resid_add(psrc, pkeys, c, t0, w):
            sc.add("dve", lambda e: e.scalar_tensor_tensor(
                out=h32[:, c, t0:t0 + w], in0=psrc, scalar=INV_ALPHA, in1=h32[:, c, t0:t0 + w],
                op0=ALU.mult, op1=ALU.add), r=pkeys + hkey(c, t0, t0 + w), w=hkey(c, t0, t0 + w))

        def load_seq(s):
            ab.reset(); af.reset()
            xs = af.alloc(2, D)
            for n in range(NTB):
                b = n % 2
                sc.add("sp", lambda e, n=n, b=b: e.dma_start(out=xs[:, b, :], in_=x_d[s, n * 128:(n + 1) * 128, :]),
                       w=[("xs", b)], dma=f"xs{b}")

                def tr(e, n=n, b=b):
                    for c in range(NCH):
                        i = e.transpose(out=ps[:, b * 1024 + c * 128: b * 1024 + (c + 1) * 128],
                                        in_=xs[:, b, c * 128:(c + 1) * 128], identity=ident)
                    return i
                sc.add("pe", tr, r=[("xs", b), "cst_f"], w=PK(2 * b, 2))
                src = ps[:, b * 1024:(b + 1) * 1024].rearrange("p (c t) -> p c t", c=NCH)
                eng = "act" if n % 2 == 0 else "dve"
                if eng == "act":
                    sc.add("act", lambda e, n=n, src=src: e.activation(out=h32[:, :, n * 128:(n + 1) * 128], in_=src,
                                                                       func=AF.Copy),
                           r=PK(2 * b, 2), w=hkeys_all(n * 128, (n + 1) * 128))
                else:
                    sc.add("dve", lambda e, n=n, src=src: e.tensor_copy(out=h32[:, :, n * 128:(n + 1) * 128], in_=src),
                           r=PK(2 * b, 2), w=hkeys_all(n * 128, (n + 1) * 128))
            sc.barrier()

        def store_seq(s):
            ab.reset(); af.reset()
            ys = af.alloc(2, D)
            for n in range(NTB):
                b = n % 2

                def tr(e, n=n, b=b):
                    for c in range(NCH):
                        i = e.transpose(out=ps[:, b * 1024 + c * 128: b * 1024 + (c + 1) * 128],
                                        in_=h32[:, c, n * 128:(n + 1) * 128], identity=ident)
                    return i
                sc.add("pe", tr, r=hkeys_all(n * 128, (n + 1) * 128) + ["cst_f"], w=PK(2 * b, 2))
                if n % 2 == 0:
                    sc.add("act", lambda e, b=b: e.activation(out=ys[:, b, :], in_=ps[:, b * 1024:(b + 1) * 1024],
                                                              func=AF.Copy), r=PK(2 * b, 2), w=[("ys", b)])
                else:
                    sc.add("dve", lambda e, b=b: e.tensor_copy(out=ys[:, b, :], in_=ps[:, b * 1024:(b + 1) * 1024]),
                           r=PK(2 * b, 2), w=[("ys", b)])
                sc.add("sp", lambda e, n=n, b=b: e.dma_start(out=y_d[s, n * 128:(n + 1) * 128, :], in_=ys[:, b, :]),
                       r=[("ys", b)], w=[("yout", s, n)], dma=f"yout{b}")
            sc.barrier()

        def ffn_layer(l):
            ab.reset(); af.reset()
            hbT = ab.alloc(NCH, FT)
            pT = ab.alloc(11, FT)
            wup = [ab.alloc(2, NCH, 128) for _ in range(3)]
            wdn = [ab.alloc(11, 128) for _ in range(2)]
            raw = [[af.alloc(FT + 2) for _ in range(2)] for _ in range(2)]
            yy = [[af.alloc(FT) for _ in range(2)] for _ in range(2)]
            hal = af.alloc(44, 2)
            mean_sb = af.alloc(512); var_sb = af.alloc(512); rstd_sb = af.alloc(512)
            xbuf = ab.alloc(NCH, 512); xsqbuf = ab.alloc(NCH, 512)
            cp = lambda k, i: convp[:, l * 176 + k * 44 + i: l * 176 + k * 44 + i + 1]
            wupv = b_up[l].rearrange("(c p) n -> p c n", p=128)
            wdnv = b_dn[l].rearrange("(k p) n -> p k n", p=128)
            nup = 0
            ndnc = [0]
            sc.add("pool", lambda e: e.memset(hal, 0.0), w=["hal"])
            def cast_hbT(ft):
                t0 = ft * FT
                for hf in range(2):
                    cs = slice(4 * hf, 4 * hf + 4)
                    hk = []
                    for c in range(4 * hf, 4 * hf + 4):
                        hk += hkey(c, t0, t0 + FT)
                    if hf == 0:
                        sc.add("act", lambda e, cs=cs: e.activation(out=hbT[:, cs, :], in_=h32[:, cs, t0:t0 + FT],
                                                                    func=AF.Copy), r=hk, w=[("hbT", hf)])
                    else:
                        sc.add("dve", lambda e, cs=cs: e.tensor_copy(out=hbT[:, cs, :], in_=h32[:, cs, t0:t0 + FT]),
                               r=hk, w=[("hbT", hf)])
            cast_hbT(0)
            tail = [None]
            ln_pending = [None]
            dn_pending = [None]
            nupt = [0]
            for ft in range(NFT):
                t0 = ft * FT
                for half in range(2):
                    for il in range(11):
                        i = half * 11 + il
                        slot = nup % 3; nup += 1
                        wt = wup[slot]
                        sc.add("sp", lambda e, wt=wt, i=i: [
                            e.dma_start(out=wt[:, 0, :, :], in_=wupv[:, :, i * 128:(i + 1) * 128]),
                            e.dma_start(out=wt[:, 1, :, :], in_=wupv[:, :, FF + i * 128:FF + (i + 1) * 128])],
                            r=castkeys[("c_up", l)], w=[("wup", slot)], dma=f"wup{slot}", ndma=2)
                        for gv in range(2):
                            bk = (nupt[0] % 3) * 2; nupt[0] += 1
                            idx = gv * NFF + i
                            rw = raw[gv][il % 2]
                            y = yy[gv][il % 2]
                            kr = ("raw", gv, il % 2); krh = ("rawh", gv, il % 2); ky = ("yy", gv, il % 2)

                            def up(e, wt=wt, gv=gv, bk=bk):
                                for c in range(NCH):
                                    for tt in range(FT // 512):
                                        ins = e.matmul(bank(bk + tt), lhsT=wt[:, gv, c, :],
                                                       rhs=hbT[:, c, tt * 512:(tt + 1) * 512],
                                                       start=(c == 0), stop=(c == NCH - 1))
                                return ins
                            sc.add("pe", up, r=[("wup", slot), ("hbT", 0), ("hbT", 1)], w=PK(bk, FT // 512))
                            psrc = ps[:, bk * 512: bk * 512 + FT]
                            sc.add("pool", lambda e, rw=rw, idx=idx: e.tensor_copy(out=rw[:, 0:2], in_=hal[:, idx, :]),
                                   r=["hal"], w=[krh])
                            sc.add("act", lambda e, rw=rw, psrc=psrc: e.activation(out=rw[:, 2:FT + 2], in_=psrc,
                                                                                    func=AF.Copy),
                                   r=PK(bk, FT // 512), w=[kr])
                            sc.add("act", lambda e, y=y, psrc=psrc, idx=idx: e.activation(
                                out=y, in_=psrc, func=AF.Identity, scale=cp(2, idx), bias=cp(3, idx)),
                                r=PK(bk, FT // 512) + ["convp"], w=[ky])
                            sc.add("dve", lambda e, y=y, rw=rw, idx=idx: e.scalar_tensor_tensor(
                                out=y, in0=rw[:, 1:FT + 1], scalar=cp(1, idx), in1=y, op0=ALU.mult, op1=ALU.add),
                                r=[kr, krh, ky, "convp"], w=[ky])
                            sc.add("dve", lambda e, y=y, rw=rw, idx=idx: e.scalar_tensor_tensor(
                                out=y, in0=rw[:, 0:FT], scalar=cp(0, idx), in1=y, op0=ALU.mult, op1=ALU.add),
                                r=[kr, krh, ky, "convp"], w=[ky])
                            sc.add("pool", lambda e, rw=rw, idx=idx: e.tensor_copy(out=hal[:, idx, :],
                                                                                  in_=rw[:, FT:FT + 2]),
                                   r=[kr], w=["hal"])
                        if il % 3 == 0:
                            drain(2)
                        if dn_pending[0] is not None and half == 1 and il == 0:
                            dn_pending[0](); dn_pending[0] = None
                        if tail[0] is not None:
                            tail[0](); tail[0] = None

                        def mk_tail(il=il):
                            yg = yy[0][il % 2]; yv = yy[1][il % 2]

                            def t():
                                sc.add("act", lambda e: e.activation(out=yg, in_=yg, func=AF.Silu),
                                       r=[("yy", 0, il % 2)], w=[("yy", 0, il % 2)])
                                sc.add("dve", lambda e: e.tensor_tensor(out=pT[:, il, :], in0=yg, in1=yv, op=ALU.mult),
                                       r=[("yy", 0, il % 2), ("yy", 1, il % 2)], w=[("pT", il)])
                            return t
                        tail[0] = mk_tail()
                        if ln_pending[0] is not None and half == 0 and il == 0:
                            ln_pending[0](); ln_pending[0] = None
                    if tail[0] is not None:
                        tail[0](); tail[0] = None
                    if half == 1 and ft + 1 < NFT:
                        cast_hbT(ft + 1)

                    def emit_dn(half=half, t0=t0):
                        for fo in range(NCH):
                            slot = ndnc[0] % 2; ndnc[0] += 1
                            wd = wdn[slot]
                            sc.add("sp", lambda e, wd=wd, fo=fo: e.dma_start(
                                out=wd, in_=wdnv[:, half * 11:(half + 1) * 11, fo * 128:(fo + 1) * 128]),
                                r=castkeys[("c_dn", l)], w=[("wdn", slot)], dma=f"wdn{slot}")
                            for tt in range(FT // 512):
                                bk = 6 + tt

                                def dn(e, wd=wd, tt=tt, bk=bk):
                                    for k in range(11):
                                        ins = e.matmul(bank(bk), lhsT=wd[:, k, :], rhs=pT[:, k, tt * 512:(tt + 1) * 512],
                                                       start=(k == 0), stop=(k == 10))
                                    return ins
                                sc.add("pe", dn, r=[("wdn", slot)] + [("pT", k) for k in range(11)], w=PK(bk))
                                resid_add(bank(bk), PK(bk), fo, t0 + tt * 512, 512)
                    if half == 0:
                        dn_pending[0] = emit_dn
                    else:
                        emit_dn()
                def do_ln(t0=t0):
                    for tt in range(FT // 512):
                        layer_norm(l, 1, t0 + tt * 512, xbuf, xsqbuf,
                                   ([("lnxb", 0), ("lnxb", 1)], [("lnxsq", 0), ("lnxsq", 1)]), (mean_sb, var_sb, rstd_sb))
                if ft + 1 < NFT:
                    ln_pending[0] = do_ln
                else:
                    do_ln()
            sc.barrier()

        def gmlp_layer(l):
            j = l // 2
            ab.reset(); af.reset()
            hbT = ab.alloc(NCH, TT)
            wv = ab.alloc(NCH, E2)
            uT = ab.alloc(16, TT)
            vln = ab.alloc(4, E2)
            vflat = ab.last_flat
            xbuf = vflat[:, 0:NCH * 512].rearrange("p (c t) -> p c t", c=NCH)
            xsqbuf = vflat[:, NCH * 512:2 * NCH * 512].rearrange("p (c t) -> p c t", c=NCH)
            wu = [ab.alloc(NCH, 128) for _ in range(2)]
            wo = [ab.alloc(16, 128) for _ in range(2)]
            bsb = ab.alloc(GG * 128)
            v32 = [af.alloc(E2) for _ in range(2)]
            Gt = af.alloc(E2); Bt = af.alloc(E2)
            bsf = af.alloc(GG * 128)
            bst = af.alloc(4 * 6); mv = af.alloc(2); rs = af.alloc(1)
            mean_sb = af.alloc(512); var_sb = af.alloc(512); rstd_sb = af.alloc(512)
            winv = b_gin[j].rearrange("(c p) n -> p c n", p=128)
            woutv = b_gout[j].rearrange("(k p) n -> p k n", p=128)
            ck_in = castkeys[("c_gin", j)]; ck_out = castkeys[("c_gout", j)]
            for q in range(4):
                sc.add("sp", lambda e, q=q: e.dma_start(out=wv[:, :, q * 512:(q + 1) * 512],
                                                        in_=winv[:, :, E2 + q * 512:E2 + (q + 1) * 512]),
                       r=ck_in, w=[("wv", q)], dma=f"wv{q}")
            sc.add("sp", lambda e: e.dma_start(out=Gt, in_=gb_d[j, 0, :, :]), w=["Gt"], dma="Gt")
            sc.add("sp", lambda e: e.dma_start(out=Bt, in_=gb_d[j, 1, :, :]), w=["Bt"], dma="Bt")
            sc.add("sp", lambda e: e.dma_start(out=bsf[0:1, :], in_=bs_d[j, :, :]), w=["bsf"], dma="bsf")
            sc.add("dve", lambda e: e.tensor_copy(out=bsb[0:1, :], in_=bsf[0:1, :]), r=["bsf"], w=["bsb"])
            nwu = 0; nwo = 0; nps = 0
            def cast_hbT(T):
                tq = T * TT
                sc.add("dve", lambda e: e.tensor_copy(out=hbT, in_=h32[:, :, tq:tq + TT]),
                       r=hkeys_all(tq, tq + TT), w=["hbT"])
            cast_hbT(0)
            for T in range(NT):
                t0 = T * TT
                def u_steps(fos):
                  nonlocal nwu, nps
                  for fo in fos:
                      slot = nwu % 2; nwu += 1
                      wt = wu[slot]
                      sc.add("sp", lambda e, wt=wt, fo=fo: e.dma_start(out=wt, in_=winv[:, :, fo * 128:(fo + 1) * 128]),
                             r=ck_in, w=[("wu", slot)], dma=f"wu{slot}")
                      bk = 4 + nps % 2; nps += 1

                      def uproj(e, wt=wt, bk=bk):
                          for c in range(NCH):
                              ins = e.matmul(bank(bk), lhsT=wt[:, c, :], rhs=hbT[:, c, :], start=(c == 0),
                                             stop=(c == NCH - 1))
                          return ins
                      sc.add("pe", uproj, r=[("wu", slot), "hbT"], w=PK(bk))
                      sc.add("act", lambda e, fo=fo, bk=bk: e.activation(out=uT[:, fo, :], in_=bank(bk),
                                                                         func=AF.Gelu_apprx_tanh),
                             r=PK(bk), w=[("uT", fo)])
                for n in range(4):
                    vb = v32[n % 2]; kv = ("v32", n % 2)
                    for hf in range(2):
                        def vproj(e, n=n, hf=hf):
                            for q in range(2 * hf, 2 * hf + 2):
                                for c in range(NCH):
                                    ins = e.matmul(bank(q), lhsT=hbT[:, c, n * 128:(n + 1) * 128],
                                                   rhs=wv[:, c, q * 512:(q + 1) * 512], start=(c == 0),
                                                   stop=(c == NCH - 1))
                            return ins
                        sc.add("pe", vproj, r=["hbT", ("wv", 2 * hf), ("wv", 2 * hf + 1)], w=PK(2 * hf, 2))
                        sc.add("act", lambda e, vb=vb, hf=hf: e.activation(
                            out=vb[:, hf * 1024:(hf + 1) * 1024], in_=ps[:, hf * 1024:(hf + 1) * 1024],
                            func=AF.Gelu_apprx_tanh), r=PK(2 * hf, 2), w=[kv + (hf,)])
                        for q in range(2 * hf, 2 * hf + 2):
                            sc.add("dve", lambda e, vb=vb, q=q: e.bn_stats(out=bst[:, q * 6:(q + 1) * 6],
                                                                           in_=vb[:, q * 512:(q + 1) * 512]),
                                   r=[kv + (hf,)], w=[("bst", q)])
                    kvs = [kv + (0,), kv + (1,)]
                    sc.add("dve", lambda e: e.bn_aggr(out=mv, in_=bst), r=[("bst", q) for q in range(4)], w=["mv"])
                    sc.add("dve", lambda e: e.tensor_scalar(out=rs, in0=mv[:, 1:2], scalar1=EPS_G, scalar2=None,
                                                            op0=ALU.add), r=["mv"], w=["rs"])
                    sc.add("act", lambda e: e.activation(out=rs, in_=rs, func=AF.Ln), r=["rs"], w=["rs"])
                    sc.add("act", lambda e: e.activation(out=rs, in_=rs, func=AF.Exp, scale=-0.5), r=["rs"], w=["rs"])
                    sc.add("dve", lambda e, vb=vb: e.scalar_tensor_tensor(out=vb, in0=vb, scalar=mv[:, 0:1], in1=Gt,
                                                                          op0=ALU.subtract, op1=ALU.mult),
                           r=kvs + ["mv", "Gt"], w=kvs)
                    sc.add("dve", lambda e, vb=vb, n=n: e.scalar_tensor_tensor(out=vln[:, n, :], in0=vb, scalar=rs,
                                                                               in1=Bt, op0=ALU.mult, op1=ALU.add),
                           r=kvs + ["rs", "Bt"], w=[("vln", n)])
                    u_steps(range(4 * n, 4 * n + 4))
                    drain(1)
                drain(4)
                if T + 1 < NT:
                    cast_hbT(T + 1)
                for cj in range(16):
                    g = cj // 2
                    bk = 4 + nps % 2; nps += 1

                    def spat(e, cj=cj, g=g, bk=bk):
                        brow = bsb[0:1, g * 128:(g + 1) * 128].unsqueeze(1).to_broadcast([1, 4, 128])
                        e.matmul(bank(bk).rearrange("p (n t) -> p n t", n=4), lhsT=cst_b[0:1, 512:640], rhs=brow,
                                 start=True, stop=False, skip_group_check=True)
                        for n in range(4):
                            ins = e.matmul(bank(bk)[:, n * 128:(n + 1) * 128], lhsT=vln[:, n, cj * 128:(cj + 1) * 128],
                                           rhs=wct[:, (j * GG + g) * 128:(j * GG + g + 1) * 128], start=False,
                                           stop=(n == 3), skip_group_check=True)
                        return ins
                    sc.add("pe", spat, r=[("vln", n) for n in range(4)] + ["bsb", "wct", "cst_b"], w=PK(bk))
                    sc.add("dve", lambda e, cj=cj, bk=bk: e.tensor_tensor(out=uT[:, cj, :], in0=bank(bk),
                                                                          in1=uT[:, cj, :], op=ALU.mult),
                           r=PK(bk) + [("uT", cj)], w=[("uT", cj)])
                for fo in range(NCH):
                    slot = nwo % 2; nwo += 1
                    wt = wo[slot]
                    sc.add("sp", lambda e, wt=wt, fo=fo: e.dma_start(out=wt, in_=woutv[:, :, fo * 128:(fo + 1) * 128]),
                           r=ck_out, w=[("wo", slot)], dma=f"wo{slot}")
                    bk = 4 + nps % 2; nps += 1

                    def oproj(e, wt=wt, bk=bk):
                        for k in range(16):
                            ins = e.matmul(bank(bk), lhsT=wt[:, k, :], rhs=uT[:, k, :], start=(k == 0), stop=(k == 15))
                        return ins
                    sc.add("pe", oproj, r=[("wo", slot)] + [("uT", k) for k in range(16)], w=PK(bk))
                    resid_add(bank(bk), PK(bk), fo, t0, 512)
                layer_norm(l, 0, t0, xbuf, xsqbuf, ([("vln", 0), ("vln", 1)], [("vln", 2), ("vln", 3)]),
                           (mean_sb, var_sb, rstd_sb))
            sc.barrier()

        def attn_layer(l):
            j = l // 2
            ab.reset(); af.reset()
            hb = ab.alloc(NCH, S)
            QK = [(ab.alloc(S), ab.alloc(S), ab.alloc(NTB, 128)) for _ in range(2)]
            oTs = [ab.alloc(S) for _ in range(2)]
            scr = ab.alloc(18, 512)
            eb = [scr[:, 0:2, :], scr[:, 2:4, :], scr[:, 16:18, :]]
            spb = [scr[:, 4:6, :], scr[:, 6:8, :]]
            gb_ = scr[:, 8:10, :]
            abuf = [scr[:, 10:12, :], scr[:, 12:14, :]]
            R = scr[:, 14:16, :]
            xbuf = scr[:, 0:8, :]; xsqbuf = scr[:, 8:16, :]
            mean_sb = af.alloc(512); var_sb = af.alloc(512); rstd_sb = af.alloc(512)
            wreg = arena_f[:, 2048:2048 + 4096].bitcast(BF16)
            Ws = []
            for p in range(2):
                base = p * 4096
                Ws.append((wreg[:, base:base + 1024].rearrange("p (a b) -> p a b", a=NCH),
                           wreg[:, base + 1024:base + 2048].rearrange("p (a b) -> p a b", a=NCH),
                           wreg[:, base + 2048:base + 3072].rearrange("p (a b) -> p a b", a=NCH),
                           wreg[:, base + 3072:base + 4096]))
            winv = b_ain[j].rearrange("(c p) n -> p c n", p=128)
            ck_in = castkeys[("c_ain", j)]; ck_out = castkeys[("c_aout", j)]
            for c in range(NCH):
                sc.add("act" if c % 2 == 0 else "dve",
                       (lambda e, c=c: e.activation(out=hb[:, c, :], in_=h32[:, c, :], func=AF.Copy)) if c % 2 == 0 else
                       (lambda e, c=c: e.tensor_copy(out=hb[:, c, :], in_=h32[:, c, :])),
                       r=hkey(c, 0, S), w=[("hb", c)])
            hbk = [("hb", c) for c in range(NCH)]
            zv = ps[:, 0:1024].rearrange("p (h n) -> p h n", h=2)
            cv = ps[:, 1024:2048].rearrange("p (h n) -> p h n", h=2)
            cnt = {"npj": 0}

            def nbank(fixed):
                if fixed is not None:
                    return fixed
                bk = 6 + cnt["npj"] % 2; cnt["npj"] += 1
                return bk

            def load_w(c, p):
                wq, wk, wvv, wo = Ws[p]
                sc.add("sp", lambda e: e.dma_start(out=wq, in_=winv[:, :, c * 128:(c + 1) * 128]),
                       r=ck_in, w=[("wq", p)], dma=f"wq{p}")
                sc.add("sp", lambda e: e.dma_start(out=wk, in_=winv[:, :, D + c * 128:D + (c + 1) * 128]),
                       r=ck_in, w=[("wk", p)], dma=f"wk{p}")
                sc.add("sp", lambda e: e.dma_start(out=wvv, in_=winv[:, :, 2 * D + c * 128:2 * D + (c + 1) * 128]),
                       r=ck_in, w=[("wvv", p)], dma=f"wvv{p}")

            def load_wo(c, p):
                wo = Ws[p][3]
                sc.add("sp", lambda e: e.dma_start(out=wo, in_=b_aout[j, c * 128:(c + 1) * 128, :]),
                       r=ck_out, w=[("wo_a", p)], dma=f"wo_a{p}")

            def proj_steps(c, p, evac_act, fixed=None):
                wq, wk, wvv, wo = Ws[p]
                qT, kT, vv = QK[p]
                steps = []
                for (wt, dst, kw, kd, scl) in ((wq, qT, ("wq", p), "qT", 0.125), (wk, kT, ("wk", p), "kT", 1.0)):
                    for tt in range(NT):
                        box = {}

                        def s1(wt=wt, tt=tt, kw=kw, box=box):
                            box["bk"] = bk = nbank(fixed)

                            def f(e):
                                for cc in range(4):
                                    ins = e.matmul(bank(bk), lhsT=wt[:, cc, :], rhs=hb[:, cc, tt * 512:(tt + 1) * 512],
                                                   start=(cc == 0), stop=False)
                                return ins
                            sc.add("pe", f, r=[kw] + hbk, w=PK(bk))

                        def s2(wt=wt, tt=tt, kw=kw, box=box, dst=dst, kd=kd, scl=scl):
                            bk = box["bk"]

                            def f(e):
                                for cc in range(4, NCH):
                                    ins = e.matmul(bank(bk), lhsT=wt[:, cc, :], rhs=hb[:, cc, tt * 512:(tt + 1) * 512],
                                                   start=False, stop=(cc == NCH - 1))
                                return ins
                            sc.add("pe", f, r=[kw] + hbk, w=PK(bk))
                            o = dst[:, tt * 512:(tt + 1) * 512]
                            if evac_act:
                                sc.add("act", lambda e: e.activation(out=o, in_=bank(bk), func=AF.Identity, scale=scl),
                                       r=PK(bk), w=[(kd, p, tt)])
                            else:
                                sc.add("dve", lambda e: e.tensor_scalar(out=o, in0=bank(bk), scalar1=scl, scalar2=None,
                                                                        op0=ALU.mult), r=PK(bk), w=[(kd, p, tt)])
                        steps += [s1, s2]
                for n4 in range(NTB // 4):
                    box = {}
                    for q in range(4):
                        def sv(n4=n4, q=q, box=box):
                            if q == 0:
                                box["bk"] = nbank(fixed)
                            bk = box["bk"]
                            n = n4 * 4 + q

                            def f(e):
                                for cc in range(NCH):
                                    ins = e.matmul(bank(bk)[:, q * 128:(q + 1) * 128],
                                                   lhsT=hb[:, cc, n * 128:(n + 1) * 128], rhs=wvv[:, cc, :],
                                                   start=(cc == 0), stop=(cc == NCH - 1))
                                return ins
                            sc.add("pe", f, r=[("wvv", p)] + hbk, w=PK(bk))
                            if q == 3:
                                sc.add("dve", lambda e: e.tensor_copy(
                                    out=vv[:, n4 * 4:(n4 + 1) * 4, :], in_=bank(bk).rearrange("p (q d) -> p q d", q=4)),
                                    r=PK(bk), w=[("vv", p, n4)])
                        steps.append(sv)
                return steps

            def outproj_steps(c, p, fixed=None):
                wo = Ws[p][3]
                oT = oTs[p]
                steps = []
                for fo in range(NCH):
                    for tt in range(NT):
                        def so(fo=fo, tt=tt):
                            bk = nbank(fixed)
                            sc.add("pe", lambda e: e.matmul(
                                bank(bk), lhsT=wo[:, fo * 128:(fo + 1) * 128], rhs=oT[:, tt * 512:(tt + 1) * 512],
                                start=True, stop=True), r=[("wo_a", p), ("oT", p, tt)], w=PK(bk))
                            resid_add(bank(bk), PK(bk), fo, tt * 512, 512)
                        steps.append(so)
                return steps

            def attention(c, p, pending):
                qT, kT, vv = QK[p]
                oT = oTs[p]
                items = []
                for T in range(NT):
                    blocks = list(range(4 * T + 3, -1, -1))
                    for i, kb in enumerate(blocks):
                        jj = kb - 4 * T
                        items.append(dict(T=T, i=i, kb=kb, c0=max(jj, 0) * 128, dg=(jj >= 0), first=(i == 0),
                                          last=(i == len(blocks) - 1)))
                G = len(items)
                per = -(-len(pending) // G) if pending else 0

                def emit_z(g):
                    it = items[g]

                    def zf(e, kb=it["kb"], c0=it["c0"], T=it["T"]):
                        for h in range(2):
                            ins = e.matmul(zv[:, h, c0:512], lhsT=kT[64 * h:64 * h + 64, kb * 128:(kb + 1) * 128],
                                           rhs=qT[64 * h:64 * h + 64, T * 512 + c0:(T + 1) * 512],
                                           start=True, stop=True)
                        return ins
                    sc.add("pe", zf, r=[("kT", p, it["kb"] // 4), ("qT", p, it["T"])], w=PK(0, 2))

                def emit_tail(g):
                    it = items[g]; c0 = it["c0"]; b = g % 2; b3 = g % 3
                    sc.add("act", lambda e: e.activation(out=gb_[:, :, c0:512], in_=cv[:, :, c0:512], func=AF.Exp),
                           r=PK(2, 2), w=["g"])
                    sc.add("dve", lambda e: e.tensor_tensor(out=abuf[b][:, :, c0:512], in0=eb[b3][:, :, c0:512],
                                                            in1=gb_[:, :, c0:512], op=ALU.mult),
                           r=["g", ("e", b3)], w=[("a", b)])

                def emit_av(g):
                    it = items[g]; c0 = it["c0"]; b = g % 2; kb = it["kb"]; first = it["first"]; T = it["T"]

                    def av(e):
                        for h in range(2):
                            ins = e.matmul(bank(4 + h)[:, c0:512], lhsT=vv[:, kb, :], rhs=abuf[b][:, h, c0:512],
                                           start=first, stop=False, skip_group_check=True)
                        return ins
                    sc.add("pe", av, r=[("a", b), ("vv", p, kb // 4)], w=PK(4, 2))
                    if it["last"]:
                        for h in range(2):
                            sc.add("dve", lambda e, h=h: e.tensor_copy(
                                out=oT[64 * h:64 * h + 64, T * 512:(T + 1) * 512], in_=bank(4 + h)[64 * h:64 * h + 64, :]),
                                r=PK(4 + h), w=[("oT", p, T)])

                def emit_fill():
                    def fill(e):
                        for _ in range(NFILL):
                            ins = e.matmul(bank(6), lhsT=negones, rhs=cst_b[:, 0:512], start=True, stop=True)
                        return ins
                    if NFILL > 0 and not pending:
                        sc.add("pe", fill, r=["cst_b"], w=PK(6))

                emit_z(0)
                for g in range(G):
                    it = items[g]; c0 = it["c0"]; b = g % 2; b3 = g % 3
                    if it["first"]:
                        sc.add("pool", lambda e: e.memset(R, 0.0), w=["R"])
                    sc.add("act", lambda e, c0=c0, b3=b3: e.activation(out=eb[b3][:, :, c0:512], in_=zv[:, :, c0:512],
                                                                       func=AF.Exp), r=PK(0, 2), w=[("e", b3)])
                    if it["dg"]:
                        mbc = mask_su.unsqueeze(1).to_broadcast([128, 2, 128])
                        sc.add("pool", lambda e, c0=c0, b3=b3, mbc=mbc: e.tensor_tensor(
                            out=eb[b3][:, :, c0:c0 + 128], in0=eb[b3][:, :, c0:c0 + 128], in1=mbc, op=ALU.mult),
                            r=[("e", b3), "cst_b"], w=[("e", b3)])
                    sc.add("act", lambda e, c0=c0, b=b, b3=b3: e.activation(out=spb[b][:, :, c0:512],
                                                                           in_=eb[b3][:, :, c0:512],
                                                                           func=AF.Ln, bias=1.0),
                           r=[("e", b3)], w=[("sp", b)])
                    if g + 1 < G:
                        emit_z(g + 1)
                    if g > 0:
                        emit_tail(g - 1)

                    def negc(e, c0=c0, b=b, first=it["first"]):
                        for h in range(2):
                            ins = e.matmul(cv[:, h, c0:512], lhsT=negtri, rhs=spb[b][:, h, c0:512], start=True,
                                           stop=first)
                            if not first:
                                ins = e.matmul(cv[:, h, c0:512], lhsT=negones, rhs=R[:, h, c0:512], start=False,
                                               stop=True)
                        return ins
                    sc.add("pe", negc, r=[("sp", b), "R", "cst_b"], w=PK(2, 2))
                    sc.add("dve", lambda e, c0=c0, b=b: e.tensor_tensor(out=R[:, :, c0:512], in0=R[:, :, c0:512],
                                                                       in1=spb[b][:, :, c0:512], op=ALU.add),
                           r=[("sp", b), "R"], w=["R"])
                    if g > 0:
                        emit_av(g - 1)
                    for _ in range(per):
                        if pending:
                            pending.pop(0)()
                    emit_fill()
                emit_tail(G - 1)
                emit_av(G - 1)
                while pending:
                    pending.pop(0)()

            if "noattn" not in dbg:
                load_w(0, 0); load_wo(0, 0)
                for stp in proj_steps(0, 0, True):
                    stp()
                for c in range(NCH):
                    p = c % 2
                    pending = []
                    if c + 1 < NCH:
                        load_w(c + 1, 1 - p)
                        pending += proj_steps(c + 1, 1 - p, False, fixed=6)
                    if c > 0:
                        opend = outproj_steps(c - 1, 1 - p, fixed=7)
                        mix = []
                        while pending or opend:
                            if pending: mix.append(pending.pop(0))
                            if pending: mix.append(pending.pop(0))
                            if opend: mix.append(opend.pop(0))
                        pending = mix
                    drain(2)
                    attention(c, p, pending)
                    if c + 1 < NCH:
                        load_wo(c + 1, 1 - p)
                for stp in outproj_steps(NCH - 1, (NCH - 1) % 2):
                    stp()
            sc.barrier()
            for T in range(NT):
                layer_norm(l, 0, T * 512, xbuf, xsqbuf, ([("lnxb", 0), ("lnxb", 1)], [("lnxsq", 0), ("lnxsq", 1)]),
                           (mean_sb, var_sb, rstd_sb))
            sc.barrier()

        for s in range(nseq):
            load_seq(s)
            for l in range(depth):
                if s == 0:
                    flush_until(2 * l)
                    cast_phase(2 * l + 2)
                if l % 2 == 0:
                    attn_layer(l)
                else:
                    gmlp_layer(l)
                if s == 0:
                    flush_until(2 * l + 1)
                    cast_phase(2 * l + 3)
                if "noffn" not in dbg:
                    ffn_layer(l)
            store_seq(s)
        sc.add("sp", None, r=[("yout", s, n) for s in range(nseq) for n in range(NTB)])
        sc.emit(nc, st)
    return nc, sc


def host_prep(inputs, depth):
    NG = depth // 2
    f = np.float32
    lnp = np.zeros((128, depth * 32), f)
    for l in range(depth):
        for which, (g, b) in enumerate(((inputs["ln_mix_g"], inputs["ln_mix_b"]), (inputs["ln_ffn_g"], inputs["ln_ffn_b"]))):
            lnp[:, l * 32 + which * 16: l * 32 + which * 16 + 8] = np.asarray(g[l], f).reshape(8, 128).T
            lnp[:, l * 32 + which * 16 + 8: l * 32 + which * 16 + 16] = np.asarray(b[l], f).reshape(8, 128).T
    convp = np.zeros((128, depth * 176), f)
    for l in range(depth):
        for k in range(3):
            convp[:, l * 176 + k * 44: l * 176 + (k + 1) * 44] = np.asarray(inputs["ffn_conv_w"][l, k], f).reshape(44, 128).T
        convp[:, l * 176 + 132: l * 176 + 176] = np.asarray(inputs["ffn_conv_b"][l], f).reshape(44, 128).T
    idx = np.arange(128)
    consts = np.zeros((128, 768), f)
    consts[:, 0:128] = np.eye(128, dtype=f)
    consts[:, 128:256] = -(idx[:, None] >= idx[None, :]).astype(f)
    consts[:, 256:384] = -1.0
    consts[:, 384:512] = (idx[:, None] < idx[None, :]).astype(f)
    consts[:, 512:640] = (idx[:, None] <= idx[None, :]).astype(f)
    consts[:, 640:768] = 1.0 / 1024.0
    ng = max(NG, 1)
    gb = np.zeros((ng, 2, 128, E2), f)
    bs = np.zeros((ng, 1, GG * 128), f)
    for j in range(NG):
        gb[j, 0] = np.broadcast_to(np.asarray(inputs["gmlp_ln_g"][j], f)[None, :], (128, E2))
        gb[j, 1] = np.broadcast_to(np.asarray(inputs["gmlp_ln_b"][j], f)[None, :], (128, E2))
        bs[j, 0] = np.asarray(inputs["gmlp_b_s"][j], f).reshape(-1)
    return {"lnp": lnp, "convp": convp, "consts": consts, "gb": gb, "bs": bs}


_CACHE = {}


def run(inputs, ncores, nseq, S, depth, ffn_t=1024, trace=False, dbg=()):
    key = (nseq, S, depth, ffn_t)
    if key not in _CACHE:
        _CACHE[key] = build(nseq, S, depth, ffn_t, dbg)
    nc, sc = _CACHE[key]
    NA = (depth + 1) // 2
    NG = max(depth // 2, 1)
    f = np.float32
    small = host_prep(inputs, depth)
    shared = {
        "attn_w_in": np.ascontiguousarray(inputs["attn_w_in"][:NA], f),
        "attn_w_out": np.ascontiguousarray(inputs["attn_w_out"][:NA], f),
        "gmlp_w_in": np.ascontiguousarray(inputs["gmlp_w_in"][:NG], f),
        "gmlp_w_out": np.ascontiguousarray(inputs["gmlp_w_out"][:NG], f),
        "ffn_w_up": np.ascontiguousarray(inputs["ffn_w_up"][:depth], f),
        "ffn_w_down": np.ascontiguousarray(inputs["ffn_w_down"][:depth], f),
        "gmlp_w_s": np.ascontiguousarray(inputs["gmlp_w_s"][:NG], f),
    }
    shared.update(small)
    x = np.asarray(inputs["x"], f)
    in_maps = []
    for i in range(ncores):
        m = dict(shared)
        m["x"] = np.ascontiguousarray(x[i * nseq:(i + 1) * nseq, :S])
        in_maps.append(m)
    res = run_bass_kernel_spmd(nc, in_maps, core_ids=list(range(ncores)), trace=trace)
    out = np.concatenate([np.asarray(r["y"]) for r in res.results], axis=0)
    return out.astype(np.float32), res


def kernel(**inputs):
    out, _ = run(inputs, 8, 4, 2048, 4)
    return out
```

```python
import contextlib
import numpy as np
import concourse.bass as bass
import concourse.mybir as mybir
from concourse.bass_utils import run_bass_kernel_spmd

F32 = mybir.dt.float32
BF16 = mybir.dt.bfloat16
AF = mybir.ActivationFunctionType
ALU = mybir.AluOpType
AX = mybir.AxisListType

D = 1024
NCH = 8
HEADS = 16
E2 = 2048
GG = 8
FF = 2816
NFF = 22
DEPTH_FULL = 4
ALPHA = (2 * DEPTH_FULL) ** 0.25
INV_ALPHA = 1.0 / ALPHA
EPS_RES = 1e-5 / (ALPHA * ALPHA)
EPS_G = 1e-5
ROT = 20000
NFILL = 2


class _Op:
    __slots__ = ("id", "eng", "fn", "deps", "dma", "ndma", "sig")


class Sched:
    ENGS = ("pe", "act", "dve", "pool", "sp")

    def __init__(self):
        self.ops = []
        self.lastw = {}
        self.readers = {}
        self.barrier_deps = set()
        self.since = {}
        self.dma_since = []

    def add(self, eng, fn, r=(), w=(), dma=None, ndma=1, nobar=False):
        o = _Op()
        o.id = len(self.ops); o.eng = eng; o.fn = fn; o.dma = dma; o.ndma = ndma; o.sig = None
        deps = set(self.barrier_deps)
        for k in r:
            if k in self.lastw:
                deps.add(self.lastw[k])
        for k in w:
            if k in self.lastw:
                deps.add(self.lastw[k])
            for rid in self.readers.get(k, {}).values():
                deps.add(rid)
        keep = set()
        for d in deps:
            p = self.ops[d]
            if p.dma is None and dma is None and p.eng == eng:
                if eng == "pe":
                    continue
                raw = any(self.lastw.get(k) == d for k in r) or any(self.lastw.get(k) == d for k in w)
                if not raw and d not in self.barrier_deps:
                    continue
            keep.add(d)
        o.deps = keep
        for k in r:
            rk = self.readers.setdefault(k, {})
            rk[eng if dma is None else ("dma", o.id)] = o.id
        for k in w:
            self.lastw[k] = o.id
            self.readers[k] = {}
        self.ops.append(o)
        if dma is None:
            self.since[eng] = o.id
        elif not nobar:
            self.dma_since.append(o.id)
        return o.id

    def barrier(self):
        self.barrier_deps = set(self.since.values()) | set(self.dma_since)
        self.dma_since = []

    def emit(self, nc, st):
        ops = self.ops
        has_dep = [False] * len(ops)
        for o in ops:
            for d in o.deps:
                has_dep[d] = True
        cnt = {e: 0 for e in self.ENGS}
        dcnt = {}
        for o in ops:
            if o.dma is None:
                if has_dep[o.id]:
                    c = cnt[o.eng]; cnt[o.eng] = c + 1
                    o.sig = (("E", o.eng, c // ROT), c % ROT + 1)
            else:
                c = dcnt.get(o.dma, 0) + o.ndma; dcnt[o.dma] = c
                o.sig = (("D", o.dma), 16 * c)
        sems = {}
        for e in self.ENGS:
            for k in range((cnt[e] + ROT - 1) // ROT):
                sems[("E", e, k)] = st.enter_context(nc.semaphore(f"s_{e}{k}"))
        for i, k in enumerate(sorted(dcnt, key=str)):
            sems[("D", k)] = st.enter_context(nc.semaphore(f"d{i}"))
        self.nsems = len(sems)
        byeng = {e: [] for e in self.ENGS}
        for o in ops:
            byeng[o.eng].append(o)

        def run(ename, eng):
            waited = {}
            for o in byeng[ename]:
                need = {}
                for d in o.deps:
                    s, v = ops[d].sig
                    if need.get(s, 0) < v:
                        need[s] = v
                for s, v in need.items():
                    if waited.get(s, 0) < v:
                        eng.wait_ge(sems[s], v)
                        waited[s] = v
                if o.fn is None:
                    continue
                ins = o.fn(eng)
                if o.dma is not None:
                    if not isinstance(ins, (list, tuple)):
                        ins = [ins]
                    assert len(ins) == o.ndma, (len(ins), o.ndma)
                    for x in ins:
                        x.then_inc(sems[("D", o.dma)], 16)
                elif has_dep[o.id]:
                    ins.then_inc(sems[o.sig[0]], 1)

        block = st.enter_context(nc.Block())

        @block.tensor
        def _(e):
            run("pe", e)

        @block.scalar
        def _(e):
            run("act", e)

        @block.vector
        def _(e):
            run("dve", e)

        @block.gpsimd
        def _(e):
            run("pool", e)

        @block.sync
        def _(e):
            run("sp", e)


def build(nseq, S, depth, ffn_t=1024, dbg=()):
    nc = bass.Bass("TRN2", target_bir_lowering=False)
    NA = (depth + 1) // 2
    NG = depth // 2
    NTB = S // 128
    TT = 512
    NT = S // TT
    FT = min(ffn_t, S)
    NFT = S // FT

    def din(name, shape, dt=F32):
        return nc.dram_tensor(name, list(shape), dt, kind="ExternalInput").ap()

    x_d = din("x", [nseq, S, D])
    y_d = nc.dram_tensor("y", [nseq, S, D], F32, kind="ExternalOutput").ap()
    w_ain = din("attn_w_in", [NA, D, 3 * D])
    w_aout = din("attn_w_out", [NA, D, D])
    w_gin = din("gmlp_w_in", [max(NG, 1), D, 2 * E2])
    w_gout = din("gmlp_w_out", [max(NG, 1), E2, D])
    w_up = din("ffn_w_up", [depth, D, 2 * FF])
    w_dn = din("ffn_w_down", [depth, FF, D])
    ws_d = din("gmlp_w_s", [max(NG, 1), GG, 128, 128])
    bs_d = din("bs", [max(NG, 1), 1, GG * 128])
    gb_d = din("gb", [max(NG, 1), 2, 128, E2])
    lnp_d = din("lnp", [128, depth * 32])
    conv_d = din("convp", [128, depth * 4 * 44])
    cst_d = din("consts", [128, 6 * 128])

    def dscr(name, shape):
        return nc.dram_tensor(name, list(shape), BF16, kind="Internal").ap()

    b_ain = dscr("b_ain", [NA, D, 3 * D])
    b_aout = dscr("b_aout", [NA, D, D])
    b_gin = dscr("b_gin", [max(NG, 1), D, 2 * E2])
    b_gout = dscr("b_gout", [max(NG, 1), E2, D])
    b_up = dscr("b_up", [depth, D, 2 * FF])
    b_dn = dscr("b_dn", [depth, FF, D])

    sc = Sched()
    st = contextlib.ExitStack()
    with st:
        E = st.enter_context
        h32 = E(nc.sbuf_tensor("h32", [128, NCH, S], F32))
        AB_N = 43 * 1024
        AF_N = 11 * 1024
        arena_b = E(nc.sbuf_tensor("arena_b", [128, AB_N], BF16))
        arena_f = E(nc.sbuf_tensor("arena_f", [128, AF_N], F32))
        cst_f = E(nc.sbuf_tensor("cst_f", [128, 6 * 128], F32))
        cst_b = E(nc.sbuf_tensor("cst_b", [128, 6 * 128], BF16))
        lnp = E(nc.sbuf_tensor("lnp_sb", [128, depth * 32], F32))
        convp = E(nc.sbuf_tensor("convp_sb", [128, depth * 176], F32))
        wct = E(nc.sbuf_tensor("wct", [128, max(NG, 1) * GG * 128], BF16))
        ps = E(nc.psum_tensor("ps", [128, 4096], F32))

        ident = cst_f[:, 0:128]
        negtri = cst_b[:, 128:256]
        negones = cst_b[:, 256:384]
        mask_su = cst_b[:, 384:512]
        mask_ui_f = cst_f[:, 512:640]
        onesd = cst_b[:, 640:768]

        class Arena:
            def __init__(self, t, n, esz):
                self.t = t; self.n = n; self.off = 0; self.esz = esz

            def reset(self):
                self.off = 0

            def alloc(self, *dims):
                n = int(np.prod(dims))
                a = self.off
                self.off = a + ((n + 31) // 32) * 32
                assert self.off <= self.n, ("arena overflow", self.off, self.n)
                v = self.t[:, a:a + n]
                self.last_flat = v
                if len(dims) == 2:
                    v = v.rearrange("p (a b) -> p a b", a=dims[0])
                elif len(dims) == 3:
                    v = v.rearrange("p (a b c) -> p a b c", a=dims[0], b=dims[1])
                return v

        ab = Arena(arena_b, AB_N, 2)
        af = Arena(arena_f, AF_N, 4)

        def bank(i, n=1):
            return ps[:, i * 512:(i + n) * 512]

        def PK(i, n=1):
            return [("ps", i + k) for k in range(n)]

        sc.add("sp", lambda e: e.dma_start(out=cst_f[:, :], in_=cst_d[:, :]), w=["cst_f"], dma="cst")
        sc.add("sp", lambda e: e.dma_start(out=lnp[:, :], in_=lnp_d[:, :]), w=["lnp"], dma="lnp")
        sc.add("sp", lambda e: e.dma_start(out=convp[:, :], in_=conv_d[:, :]), w=["convp"], dma="convp")
        sc.add("dve", lambda e: e.tensor_copy(out=cst_b[:, :], in_=cst_f[:, :]), r=["cst_f"], w=["cst_b"])

        cast_q = []
        castkeys = {}
        cur_ph = [0]

        def cast_all(dst, src, key, rows, step=256):
            keys = []
            for r0 in range(0, rows, step):
                r1 = min(rows, r0 + step)
                k = key + (r0,)
                keys.append(k)

                def emit(d=dst, s_=src, a=r0, b=r1, k=k, key=key):
                    sc.add("pool", lambda e: e.dma_start(out=d[a:b, :], in_=s_[a:b, :]),
                           w=[k], dma="wc_%s_%d" % key, nobar=True)
                cast_q.append((cur_ph[0], emit))
            castkeys[key] = keys

        def drain(n):
            for _ in range(n):
                if cast_q:
                    cast_q.pop(0)[1]()

        def flush_until(ph):
            while cast_q and cast_q[0][0] <= ph:
                cast_q.pop(0)[1]()

        def cast_phase(ph):
            if ph >= 2 * depth:
                return
            cur_ph[0] = ph
            l = ph // 2; j = l // 2
            if ph % 2 == 0:
                if l % 2 == 0:
                    cast_all(b_ain[j], w_ain[j], ("c_ain", j), D)
                    cast_all(b_aout[j], w_aout[j], ("c_aout", j), D)
                else:
                    cast_all(b_gin[j], w_gin[j], ("c_gin", j), D)
                    cast_all(b_gout[j], w_gout[j], ("c_gout", j), E2)
            else:
                cast_all(b_up[l], w_up[l], ("c_up", l), D)
                cast_all(b_dn[l], w_dn[l], ("c_dn", l), FF)

        cast_phase(0)
        cast_phase(1)
        flush_until(1)

        ab.reset(); af.reset()
        if NG > 0:
            wst = arena_f[:, AF_N - 256:AF_N].rearrange("p (a b) -> p a b", a=2)
            for j in range(NG):
                for g in range(GG):
                    b = (j * GG + g) % 2
                    sc.add("sp", lambda e, j=j, g=g, b=b: e.dma_start(out=wst[:, b, :], in_=ws_d[j, g, :, :]),
                           w=[("wst", b)], dma=f"wst{b}")
                    sc.add("pe", lambda e, b=b: e.transpose(out=bank(b)[:, 0:128], in_=wst[:, b, :], identity=ident),
                           r=[("wst", b), "cst_f"], w=PK(b))
                    sc.add("dve", lambda e, j=j, g=g, b=b: e.tensor_tensor(
                        out=wct[:, (j * GG + g) * 128:(j * GG + g + 1) * 128], in0=bank(b)[:, 0:128],
                        in1=mask_ui_f, op=ALU.mult), r=PK(b) + ["cst_f"], w=["wct"])

        def hkey(c, t0, t1):
            return [("h", c, k) for k in range(t0 // 512, (t1 + 511) // 512)]

        def hkeys_all(t0, t1):
            out = []
            for c in range(NCH):
                out += hkey(c, t0, t1)
            return out

        def layer_norm(li, which, t0, xb, xsq, kxb, stats):
            mean_sb, var_sb, rstd_sb = stats
            W = 512
            g_off = li * 32 + which * 16
            for hf in range(2):
                cs = slice(4 * hf, 4 * hf + 4)
                hk = []
                for c in range(4 * hf, 4 * hf + 4):
                    hk += hkey(c, t0, t0 + W)
                sc.add("dve", lambda e, cs=cs: e.tensor_copy(out=xb[:, cs, :], in_=h32[:, cs, t0:t0 + W]),
                       r=hk, w=[kxb[0][hf]])
                sc.add("act", lambda e, cs=cs: e.activation(out=xsq[:, cs, :], in_=h32[:, cs, t0:t0 + W],
                                                            func=AF.Square), r=hk, w=[kxb[1][hf]])
            for hf in range(2):
                def mm_stats(e, hf=hf, dst=bank(6), src=xb):
                    for c in range(4 * hf, 4 * hf + 4):
                        i = e.matmul(dst, lhsT=onesd, rhs=src[:, c, :], start=(c == 0), stop=(c == NCH - 1))
                    return i
                sc.add("pe", mm_stats, r=[kxb[0][hf], "cst_b"], w=PK(6))
            for hf in range(2):
                def mm_stats2(e, hf=hf, dst=bank(7), src=xsq):
                    for c in range(4 * hf, 4 * hf + 4):
                        i = e.matmul(dst, lhsT=onesd, rhs=src[:, c, :], start=(c == 0), stop=(c == NCH - 1))
                    return i
                sc.add("pe", mm_stats2, r=[kxb[1][hf], "cst_b"], w=PK(7))
            sc.add("act", lambda e: e.activation(out=mean_sb, in_=bank(6), func=AF.Copy), r=PK(6), w=["ln_mean"])
            sc.add("dve", lambda e: e.tensor_tensor(out=var_sb, in0=mean_sb, in1=mean_sb, op=ALU.mult),
                   r=["ln_mean"], w=["ln_var"])
            sc.add("dve", lambda e: e.scalar_tensor_tensor(out=var_sb, in0=bank(7), scalar=EPS_RES, in1=var_sb,
                                                           op0=ALU.add, op1=ALU.subtract),
                   r=PK(7) + ["ln_var"], w=["ln_var"])
            sc.add("act", lambda e: e.activation(out=var_sb, in_=var_sb, func=AF.Ln), r=["ln_var"], w=["ln_var"])
            sc.add("act", lambda e: e.activation(out=rstd_sb, in_=var_sb, func=AF.Exp, scale=-0.5),
                   r=["ln_var"], w=["ln_rstd"])
            order = [7, 0, 1, 2, 3, 4, 5, 6]
            for c in order:
                eng = "pool" if c >= 7 else "dve"
                hc = h32[:, c, t0:t0 + W]
                kc = hkey(c, t0, t0 + W)
                sc.add(eng, lambda e, hc=hc: e.tensor_tensor(out=hc, in0=hc, in1=mean_sb, op=ALU.subtract),
                       r=kc + ["ln_mean"], w=kc)
                sc.add(eng, lambda e, hc=hc: e.tensor_tensor(out=hc, in0=hc, in1=rstd_sb, op=ALU.mult),
                       r=kc + ["ln_rstd"], w=kc)
            for c in order:
                sc.add("act", lambda e, c=c: e.activation(
                    out=h32[:, c, t0:t0 + W], in_=h32[:, c, t0:t0 + W], func=AF.Identity,
                    scale=lnp[:, g_off + c:g_off + c + 1], bias=lnp[:, g_off + 8 + c:g_off + 9 + c]),
                    r=hkey(c, t0, t0 + W) + ["lnp"], w=hkey(c, t0, t0 + W))

        def resid_add(psrc, pkeys, c, t0, w):
            sc.add("dve", lambda e: e.scalar_tensor_tensor(
                out=h32[:, c, t0:t0 + w], in0=psrc, scalar=INV_ALPHA, in1=h32[:, c, t0:t0 + w],
                op0=ALU.mult, op1=ALU.add), r=pkeys + hkey(c, t0, t0 + w), w=hkey(c, t0, t0 + w))

        def load_seq(s):
            ab.reset(); af.reset()
            xs = af.alloc(2, D)
            for n in range(NTB):
                b = n % 2
                sc.add("sp", lambda e, n=n, b=b: e.dma_start(out=xs[:, b, :], in_=x_d[s, n * 128:(n + 1) * 128, :]),
                       w=[("xs", b)], dma=f"xs{b}")

                def tr(e, n=n, b=b):
                    for c in range(NCH):
                        i = e.transpose(out=ps[:, b * 1024 + c * 128: b * 1024 + (c + 1) * 128],
                                        in_=xs[:, b, c * 128:(c + 1) * 128], identity=ident)
                    return i
                sc.add("pe", tr, r=[("xs", b), "cst_f"], w=PK(2 * b, 2))
                src = ps[:, b * 1024:(b + 1) * 1024].rearrange("p (c t) -> p c t", c=NCH)
                eng = "act" if n % 2 == 0 else "dve"
                if eng == "act":
                    sc.add("act", lambda e, n=n, src=src: e.activation(out=h32[:, :, n * 128:(n + 1) * 128], in_=src,
                                                                       func=AF.Copy),
                           r=PK(2 * b, 2), w=hkeys_all(n * 128, (n + 1) * 128))
                else:
                    sc.add("dve", lambda e, n=n, src=src: e.tensor_copy(out=h32[:, :, n * 128:(n + 1) * 128], in_=src),
                           r=PK(2 * b, 2), w=hkeys_all(n * 128, (n + 1) * 128))
            sc.barrier()

        def store_seq(s):
            ab.reset(); af.reset()
            ys = af.alloc(2, D)
            for n in range(NTB):
                b = n % 2

                def tr(e, n=n, b=b):
                    for c in range(NCH):
                        i = e.transpose(out=ps[:, b * 1024 + c * 128: b * 1024 + (c + 1) * 128],
                                        in_=h32[:, c, n * 128:(n + 1) * 128], identity=ident)
                    return i
                sc.add("pe", tr, r=hkeys_all(n * 128, (n + 1) * 128) + ["cst_f"], w=PK(2 * b, 2))
                if n % 2 == 0:
                    sc.add("act", lambda e, b=b: e.activation(out=ys[:, b, :], in_=ps[:, b * 1024:(b + 1) * 1024],
                                                              func=AF.Copy), r=PK(2 * b, 2), w=[("ys", b)])
                else:
                    sc.add("dve", lambda e, b=b: e.tensor_copy(out=ys[:, b, :], in_=ps[:, b * 1024:(b + 1) * 1024]),
                           r=PK(2 * b, 2), w=[("ys", b)])
                sc.add("sp", lambda e, n=n, b=b: e.dma_start(out=y_d[s, n * 128:(n + 1) * 128, :], in_=ys[:, b, :]),
                       r=[("ys", b)], w=[("yout", s, n)], dma=f"yout{b}")
            sc.barrier()

        def ffn_layer(l):
            ab.reset(); af.reset()
            hbT = ab.alloc(NCH, FT)
            pT = ab.alloc(11, FT)
            wup = [ab.alloc(2, NCH, 128) for _ in range(3)]
            wdn = [ab.alloc(11, 128) for _ in range(2)]
            raw = [[af.alloc(FT + 2) for _ in range(2)] for _ in range(2)]
            yy = [[af.alloc(FT) for _ in range(2)] for _ in range(2)]
            hal = af.alloc(44, 2)
            mean_sb = af.alloc(512); var_sb = af.alloc(512); rstd_sb = af.alloc(512)
            xbuf = ab.alloc(NCH, 512); xsqbuf = ab.alloc(NCH, 512)
            cp = lambda k, i: convp[:, l * 176 + k * 44 + i: l * 176 + k * 44 + i + 1]
            wupv = b_up[l].rearrange("(c p) n -> p c n", p=128)
            wdnv = b_dn[l].rearrange("(k p) n -> p k n", p=128)
            nup = 0
            ndnc = [0]
            sc.add("pool", lambda e: e.memset(hal, 0.0), w=["hal"])
            def cast_hbT(ft):
                t0 = ft * FT
                for hf in range(2):
                    cs = slice(4 * hf, 4 * hf + 4)
                    hk = []
                    for c in range(4 * hf, 4 * hf + 4):
                        hk += hkey(c, t0, t0 + FT)
                    if hf == 0:
                        sc.add("act", lambda e, cs=cs: e.activation(out=hbT[:, cs, :], in_=h32[:, cs, t0:t0 + FT],
                                                                    func=AF.Copy), r=hk, w=[("hbT", hf)])
                    else:
                        sc.add("dve", lambda e, cs=cs: e.tensor_copy(out=hbT[:, cs, :], in_=h32[:, cs, t0:t0 + FT]),
                               r=hk, w=[("hbT", hf)])
            cast_hbT(0)
            tail = [None]
            ln_pending = [None]
            dn_pending = [None]
            nupt = [0]
            for ft in range(NFT):
                t0 = ft * FT
                for half in range(2):
                    for il in range(11):
                        i = half * 11 + il
                        slot = nup % 3; nup += 1
                        wt = wup[slot]
                        sc.add("sp", lambda e, wt=wt, i=i: [
                            e.dma_start(out=wt[:, 0, :, :], in_=wupv[:, :, i * 128:(i + 1) * 128]),
                            e.dma_start(out=wt[:, 1, :, :], in_=wupv[:, :, FF + i * 128:FF + (i + 1) * 128])],
                            r=castkeys[("c_up", l)], w=[("wup", slot)], dma=f"wup{slot}", ndma=2)
                        for gv in range(2):
                            bk = (nupt[0] % 3) * 2; nupt[0] += 1
                            idx = gv * NFF + i
                            rw = raw[gv][il % 2]
                            y = yy[gv][il % 2]
                            kr = ("raw", gv, il % 2); krh = ("rawh", gv, il % 2); ky = ("yy", gv, il % 2)

                            def up(e, wt=wt, gv=gv, bk=bk):
                                for c in range(NCH):
                                    for tt in range(FT // 512):
                                        ins = e.matmul(bank(bk + tt), lhsT=wt[:, gv, c, :],
                                                       rhs=hbT[:, c, tt * 512:(tt + 1) * 512],
                                                       start=(c == 0), stop=(c == NCH - 1))
                                return ins
                            sc.add("pe", up, r=[("wup", slot), ("hbT", 0), ("hbT", 1)], w=PK(bk, FT // 512))
                            psrc = ps[:, bk * 512: bk * 512 + FT]
                            sc.add("pool", lambda e, rw=rw, idx=idx: e.tensor_copy(out=rw[:, 0:2], in_=hal[:, idx, :]),
                                   r=["hal"], w=[krh])
                            sc.add("act", lambda e, rw=rw, psrc=psrc: e.activation(out=rw[:, 2:FT + 2], in_=psrc,
                                                                                    func=AF.Copy),
                                   r=PK(bk, FT // 512), w=[kr])
                            sc.add("act", lambda e, y=y, psrc=psrc, idx=idx: e.activation(
                                out=y, in_=psrc, func=AF.Identity, scale=cp(2, idx), bias=cp(3, idx)),
                                r=PK(bk, FT // 512) + ["convp"], w=[ky])
                            sc.add("dve", lambda e, y=y, rw=rw, idx=idx: e.scalar_tensor_tensor(
                                out=y, in0=rw[:, 1:FT + 1], scalar=cp(1, idx), in1=y, op0=ALU.mult, op1=ALU.add),
                                r=[kr, krh, ky, "convp"], w=[ky])
                            sc.add("dve", lambda e, y=y, rw=rw, idx=idx: e.scalar_tensor_tensor(
                                out=y, in0=rw[:, 0:FT], scalar=cp(0, idx), in1=y, op0=ALU.mult, op1=ALU.add),
                                r=[kr, krh, ky, "convp"], w=[ky])
                            sc.add("pool", lambda e, rw=rw, idx=idx: e.tensor_copy(out=hal[:, idx, :],
                                                                                  in_=rw[:, FT:FT + 2]),
                                   r=[kr], w=["hal"])
                        if il % 3 == 0:
                            drain(2)
                        if dn_pending[0] is not None and half == 1 and il == 0:
                            dn_pending[0](); dn_pending[0] = None
                        if tail[0] is not None:
                            tail[0](); tail[0] = None

                        def mk_tail(il=il):
                            yg = yy[0][il % 2]; yv = yy[1][il % 2]

                            def t():
                                sc.add("act", lambda e: e.activation(out=yg, in_=yg, func=AF.Silu),
                                       r=[("yy", 0, il % 2)], w=[("yy", 0, il % 2)])
                                sc.add("dve", lambda e: e.tensor_tensor(out=pT[:, il, :], in0=yg, in1=yv, op=ALU.mult),
                                       r=[("yy", 0, il % 2), ("yy", 1, il % 2)], w=[("pT", il)])
                            return t
                        tail[0] = mk_tail()
                        if ln_pending[0] is not None and half == 0 and il == 0:
                            ln_pending[0](); ln_pending[0] = None
                    if tail[0] is not None:
                        tail[0](); tail[0] = None
                    if half == 1 and ft + 1 < NFT:
                        cast_hbT(ft + 1)

                    def emit_dn(half=half, t0=t0):
                        for fo in range(NCH):
                            slot = ndnc[0] % 2; ndnc[0] += 1
                            wd = wdn[slot]
                            sc.add("sp", lambda e, wd=wd, fo=fo: e.dma_start(
                                out=wd, in_=wdnv[:, half * 11:(half + 1) * 11, fo * 128:(fo + 1) * 128]),
                                r=castkeys[("c_dn", l)], w=[("wdn", slot)], dma=f"wdn{slot}")
                            for tt in range(FT // 512):
                                bk = 6 + tt

                                def dn(e, wd=wd, tt=tt, bk=bk):
                                    for k in range(11):
                                        ins = e.matmul(bank(bk), lhsT=wd[:, k, :], rhs=pT[:, k, tt * 512:(tt + 1) * 512],
                                                       start=(k == 0), stop=(k == 10))
                                    return ins
                                sc.add("pe", dn, r=[("wdn", slot)] + [("pT", k) for k in range(11)], w=PK(bk))
                                resid_add(bank(bk), PK(bk), fo, t0 + tt * 512, 512)
                    if half == 0:
                        dn_pending[0] = emit_dn
                    else:
                        emit_dn()
                def do_ln(t0=t0):
                    for tt in range(FT // 512):
                        layer_norm(l, 1, t0 + tt * 512, xbuf, xsqbuf,
                                   ([("lnxb", 0), ("lnxb", 1)], [("lnxsq", 0), ("lnxsq", 1)]), (mean_sb, var_sb, rstd_sb))
                if ft + 1 < NFT:
                    ln_pending[0] = do_ln
                else:
                    do_ln()
            sc.barrier()

        def gmlp_layer(l):
            j = l // 2
            ab.reset(); af.reset()
            hbT = ab.alloc(NCH, TT)
            wv = ab.alloc(NCH, E2)
            uT = ab.alloc(16, TT)
            vln = ab.alloc(4, E2)
            vflat = ab.last_flat
            xbuf = vflat[:, 0:NCH * 512].rearrange("p (c t) -> p c t", c=NCH)
            xsqbuf = vflat[:, NCH * 512:2 * NCH * 512].rearrange("p (c t) -> p c t", c=NCH)
            wu = [ab.alloc(NCH, 128) for _ in range(2)]
            wo = [ab.alloc(16, 128) for _ in range(2)]
            bsb = ab.alloc(GG * 128)
            v32 = [af.alloc(E2) for _ in range(2)]
            Gt = af.alloc(E2); Bt = af.alloc(E2)
            bsf = af.alloc(GG * 128)
            bst = af.alloc(4 * 6); mv = af.alloc(2); rs = af.alloc(1)
            mean_sb = af.alloc(512); var_sb = af.alloc(512); rstd_sb = af.alloc(512)
            winv = b_gin[j].rearrange("(c p) n -> p c n", p=128)
            woutv = b_gout[j].rearrange("(k p) n -> p k n", p=128)
            ck_in = castkeys[("c_gin", j)]; ck_out = castkeys[("c_gout", j)]
            for q in range(4):
                sc.add("sp", lambda e, q=q: e.dma_start(out=wv[:, :, q * 512:(q + 1) * 512],
                                                        in_=winv[:, :, E2 + q * 512:E2 + (q + 1) * 512]),
                       r=ck_in, w=[("wv", q)], dma=f"wv{q}")
            sc.add("sp", lambda e: e.dma_start(out=Gt, in_=gb_d[j, 0, :, :]), w=["Gt"], dma="Gt")
            sc.add("sp", lambda e: e.dma_start(out=Bt, in_=gb_d[j, 1, :, :]), w=["Bt"], dma="Bt")
            sc.add("sp", lambda e: e.dma_start(out=bsf[0:1, :], in_=bs_d[j, :, :]), w=["bsf"], dma="bsf")
            sc.add("dve", lambda e: e.tensor_copy(out=bsb[0:1, :], in_=bsf[0:1, :]), r=["bsf"], w=["bsb"])
            nwu = 0; nwo = 0; nps = 0
            def cast_hbT(T):
                tq = T * TT
                sc.add("dve", lambda e: e.tensor_copy(out=hbT, in_=h32[:, :, tq:tq + TT]),
                       r=hkeys_all(tq, tq + TT), w=["hbT"])
            cast_hbT(0)
            for T in range(NT):
                t0 = T * TT
                def u_steps(fos):
                  nonlocal nwu, nps
                  for fo in fos:
                      slot = nwu % 2; nwu += 1
                      wt = wu[slot]
                      sc.add("sp", lambda e, wt=wt, fo=fo: e.dma_start(out=wt, in_=winv[:, :, fo * 128:(fo + 1) * 128]),
                             r=ck_in, w=[("wu", slot)], dma=f"wu{slot}")
                      bk = 4 + nps % 2; nps += 1

                      def uproj(e, wt=wt, bk=bk):
                          for c in range(NCH):
                              ins = e.matmul(bank(bk), lhsT=wt[:, c, :], rhs=hbT[:, c, :], start=(c == 0),
                                             stop=(c == NCH - 1))
                          return ins
                      sc.add("pe", uproj, r=[("wu", slot), "hbT"], w=PK(bk))
                      sc.add("act", lambda e, fo=fo, bk=bk: e.activation(out=uT[:, fo, :], in_=bank(bk),
                                                                         func=AF.Gelu_apprx_tanh),
                             r=PK(bk), w=[("uT", fo)])
                for n in range(4):
                    vb = v32[n % 2]; kv = ("v32", n % 2)
                    for hf in range(2):
                        def vproj(e, n=n, hf=hf):
                            for q in range(2 * hf, 2 * hf + 2):
                                for c in range(NCH):
                                    ins = e.matmul(bank(q), lhsT=hbT[:, c, n * 128:(n + 1) * 128],
                                                   rhs=wv[:, c, q * 512:(q + 1) * 512], start=(c == 0),
                                                   stop=(c == NCH - 1))
                            return ins
                        sc.add("pe", vproj, r=["hbT", ("wv", 2 * hf), ("wv", 2 * hf + 1)], w=PK(2 * hf, 2))
                        sc.add("act", lambda e, vb=vb, hf=hf: e.activation(
                            out=vb[:, hf * 1024:(hf + 1) * 1024], in_=ps[:, hf * 1024:(hf + 1) * 1024],
                            func=AF.Gelu_apprx_tanh), r=PK(2 * hf, 2), w=[kv + (hf,)])
                        for q in range(2 * hf, 2 * hf + 2):
                            sc.add("dve", lambda e, vb=vb, q=q: e.bn_stats(out=bst[:, q * 6:(q + 1) * 6],
                                                                           in_=vb[:, q * 512:(q + 1) * 512]),
                                   r=[kv + (hf,)], w=[("bst", q)])
                        u_steps(range(4 * n + 2 * hf, 4 * n + 2 * hf + 2))
                    kvs = [kv + (0,), kv + (1,)]
                    sc.add("dve", lambda e: e.bn_aggr(out=mv, in_=bst), r=[("bst", q) for q in range(4)], w=["mv"])
                    sc.add("dve", lambda e: e.tensor_scalar(out=rs, in0=mv[:, 1:2], scalar1=EPS_G, scalar2=None,
                                                            op0=ALU.add), r=["mv"], w=["rs"])
                    sc.add("act", lambda e: e.activation(out=rs, in_=rs, func=AF.Ln), r=["rs"], w=["rs"])
                    sc.add("act", lambda e: e.activation(out=rs, in_=rs, func=AF.Exp, scale=-0.5), r=["rs"], w=["rs"])
                    sc.add("dve", lambda e, vb=vb: e.scalar_tensor_tensor(out=vb, in0=vb, scalar=mv[:, 0:1], in1=Gt,
                                                                          op0=ALU.subtract, op1=ALU.mult),
                           r=kvs + ["mv", "Gt"], w=kvs)
                    sc.add("dve", lambda e, vb=vb, n=n: e.scalar_tensor_tensor(out=vln[:, n, :], in0=vb, scalar=rs,
                                                                               in1=Bt, op0=ALU.mult, op1=ALU.add),
                           r=kvs + ["rs", "Bt"], w=[("vln", n)])
                    drain(1)
                drain(4)
                if T + 1 < NT:
                    cast_hbT(T + 1)
                for cj in range(16):
                    g = cj // 2
                    bk = 4 + nps % 2; nps += 1

                    def spat(e, cj=cj, g=g, bk=bk):
                        brow = bsb[0:1, g * 128:(g + 1) * 128].unsqueeze(1).to_broadcast([1, 4, 128])
                        e.matmul(bank(bk).rearrange("p (n t) -> p n t", n=4), lhsT=cst_b[0:1, 512:640], rhs=brow,
                                 start=True, stop=False, skip_group_check=True)
                        for n in range(4):
                            ins = e.matmul(bank(bk)[:, n * 128:(n + 1) * 128], lhsT=vln[:, n, cj * 128:(cj + 1) * 128],
                                           rhs=wct[:, (j * GG + g) * 128:(j * GG + g + 1) * 128], start=False,
                                           stop=(n == 3), skip_group_check=True)
                        return ins
                    sc.add("pe", spat, r=[("vln", n) for n in range(4)] + ["bsb", "wct", "cst_b"], w=PK(bk))
                    sc.add("dve", lambda e, cj=cj, bk=bk: e.tensor_tensor(out=uT[:, cj, :], in0=bank(bk),
                                                                          in1=uT[:, cj, :], op=ALU.mult),
                           r=PK(bk) + [("uT", cj)], w=[("uT", cj)])
                for fo in range(NCH):
                    slot = nwo % 2; nwo += 1
                    wt = wo[slot]
                    sc.add("sp", lambda e, wt=wt, fo=fo: e.dma_start(out=wt, in_=woutv[:, :, fo * 128:(fo + 1) * 128]),
                           r=ck_out, w=[("wo", slot)], dma=f"wo{slot}")
                    bk = 4 + nps % 2; nps += 1

                    def oproj(e, wt=wt, bk=bk):
                        for k in range(16):
                            ins = e.matmul(bank(bk), lhsT=wt[:, k, :], rhs=uT[:, k, :], start=(k == 0), stop=(k == 15))
                        return ins
                    sc.add("pe", oproj, r=[("wo", slot)] + [("uT", k) for k in range(16)], w=PK(bk))
                    resid_add(bank(bk), PK(bk), fo, t0, 512)
                layer_norm(l, 0, t0, xbuf, xsqbuf, ([("vln", 0), ("vln", 1)], [("vln", 2), ("vln", 3)]),
                           (mean_sb, var_sb, rstd_sb))
            sc.barrier()

        def attn_layer(l):
            j = l // 2
            ab.reset(); af.reset()
            hb = ab.alloc(NCH, S)
            QK = [(ab.alloc(S), ab.alloc(S), ab.alloc(NTB, 128)) for _ in range(2)]
            oTs = [ab.alloc(S) for _ in range(2)]
            scr = ab.alloc(18, 512)
            eb = [scr[:, 0:2, :], scr[:, 2:4, :], scr[:, 16:18, :]]
            spb = [scr[:, 4:6, :], scr[:, 6:8, :]]
            gb_ = scr[:, 8:10, :]
            abuf = [scr[:, 10:12, :], scr[:, 12:14, :]]
            R = scr[:, 14:16, :]
            xbuf = scr[:, 0:8, :]; xsqbuf = scr[:, 8:16, :]
            mean_sb = af.alloc(512); var_sb = af.alloc(512); rstd_sb = af.alloc(512)
            wreg = arena_f[:, 2048:2048 + 4096].bitcast(BF16)
            Ws = []
            for p in range(2):
                base = p * 4096
                Ws.append((wreg[:, base:base + 1024].rearrange("p (a b) -> p a b", a=NCH),
                           wreg[:, base + 1024:base + 2048].rearrange("p (a b) -> p a b", a=NCH),
                           wreg[:, base + 2048:base + 3072].rearrange("p (a b) -> p a b", a=NCH),
                           wreg[:, base + 3072:base + 4096]))
            winv = b_ain[j].rearrange("(c p) n -> p c n", p=128)
            ck_in = castkeys[("c_ain", j)]; ck_out = castkeys[("c_aout", j)]
            for c in range(NCH):
                sc.add("act" if c % 2 == 0 else "dve",
                       (lambda e, c=c: e.activation(out=hb[:, c, :], in_=h32[:, c, :], func=AF.Copy)) if c % 2 == 0 else
                       (lambda e, c=c: e.tensor_copy(out=hb[:, c, :], in_=h32[:, c, :])),
                       r=hkey(c, 0, S), w=[("hb", c)])
            hbk = [("hb", c) for c in range(NCH)]
            zv = ps[:, 0:1024].rearrange("p (h n) -> p h n", h=2)
            cv = ps[:, 1024:2048].rearrange("p (h n) -> p h n", h=2)
            cnt = {"npj": 0}

            def nbank(fixed):
                if fixed is not None:
                    return fixed
                bk = 6 + cnt["npj"] % 2; cnt["npj"] += 1
                return bk

            def load_w(c, p):
                wq, wk, wvv, wo = Ws[p]
                sc.add("sp", lambda e: e.dma_start(out=wq, in_=winv[:, :, c * 128:(c + 1) * 128]),
                       r=ck_in, w=[("wq", p)], dma=f"wq{p}")
                sc.add("sp", lambda e: e.dma_start(out=wk, in_=winv[:, :, D + c * 128:D + (c + 1) * 128]),
                       r=ck_in, w=[("wk", p)], dma=f"wk{p}")
                sc.add("sp", lambda e: e.dma_start(out=wvv, in_=winv[:, :, 2 * D + c * 128:2 * D + (c + 1) * 128]),
                       r=ck_in, w=[("wvv", p)], dma=f"wvv{p}")

            def load_wo(c, p):
                wo = Ws[p][3]
                sc.add("sp", lambda e: e.dma_start(out=wo, in_=b_aout[j, c * 128:(c + 1) * 128, :]),
                       r=ck_out, w=[("wo_a", p)], dma=f"wo_a{p}")

            def proj_steps(c, p, evac_act, fixed=None):
                wq, wk, wvv, wo = Ws[p]
                qT, kT, vv = QK[p]
                steps = []
                for (wt, dst, kw, kd, scl) in ((wq, qT, ("wq", p), "qT", 0.125), (wk, kT, ("wk", p), "kT", 1.0)):
                    for tt in range(NT):
                        box = {}

                        def s1(wt=wt, tt=tt, kw=kw, box=box):
                            box["bk"] = bk = nbank(fixed)

                            def f(e):
                                for cc in range(4):
                                    ins = e.matmul(bank(bk), lhsT=wt[:, cc, :], rhs=hb[:, cc, tt * 512:(tt + 1) * 512],
                                                   start=(cc == 0), stop=False)
                                return ins
                            sc.add("pe", f, r=[kw] + hbk, w=PK(bk))

                        def s2(wt=wt, tt=tt, kw=kw, box=box, dst=dst, kd=kd, scl=scl):
                            bk = box["bk"]

                            def f(e):
                                for cc in range(4, NCH):
                                    ins = e.matmul(bank(bk), lhsT=wt[:, cc, :], rhs=hb[:, cc, tt * 512:(tt + 1) * 512],
                                                   start=False, stop=(cc == NCH - 1))
                                return ins
                            sc.add("pe", f, r=[kw] + hbk, w=PK(bk))
                            o = dst[:, tt * 512:(tt + 1) * 512]
                            if evac_act:
                                sc.add("act", lambda e: e.activation(out=o, in_=bank(bk), func=AF.Identity, scale=scl),
                                       r=PK(bk), w=[(kd, p, tt)])
                            else:
                                sc.add("dve", lambda e: e.tensor_scalar(out=o, in0=bank(bk), scalar1=scl, scalar2=None,
                                                                        op0=ALU.mult), r=PK(bk), w=[(kd, p, tt)])
                        steps += [s1, s2]
                for n4 in range(NTB // 4):
                    box = {}
                    for q in range(4):
                        def sv(n4=n4, q=q, box=box):
                            if q == 0:
                                box["bk"] = nbank(fixed)
                            bk = box["bk"]
                            n = n4 * 4 + q

                            def f(e):
                                for cc in range(NCH):
                                    ins = e.matmul(bank(bk)[:, q * 128:(q + 1) * 128],
                                                   lhsT=hb[:, cc, n * 128:(n + 1) * 128], rhs=wvv[:, cc, :],
                                                   start=(cc == 0), stop=(cc == NCH - 1))
                                return ins
                            sc.add("pe", f, r=[("wvv", p)] + hbk, w=PK(bk))
                            if q == 3:
                                sc.add("dve", lambda e: e.tensor_copy(
                                    out=vv[:, n4 * 4:(n4 + 1) * 4, :], in_=bank(bk).rearrange("p (q d) -> p q d", q=4)),
                                    r=PK(bk), w=[("vv", p, n4)])
                        steps.append(sv)
                return steps

            def outproj_steps(c, p, fixed=None):
                wo = Ws[p][3]
                oT = oTs[p]
                steps = []
                for fo in range(NCH):
                    for tt in range(NT):
                        def so(fo=fo, tt=tt):
                            bk = nbank(fixed)
                            sc.add("pe", lambda e: e.matmul(
                                bank(bk), lhsT=wo[:, fo * 128:(fo + 1) * 128], rhs=oT[:, tt * 512:(tt + 1) * 512],
                                start=True, stop=True), r=[("wo_a", p), ("oT", p, tt)], w=PK(bk))
                            resid_add(bank(bk), PK(bk), fo, tt * 512, 512)
                        steps.append(so)
                return steps

            def attention(c, p, pending):
                qT, kT, vv = QK[p]
                oT = oTs[p]
                items = []
                for T in range(NT):
                    blocks = list(range(4 * T + 3, -1, -1))
                    for i, kb in enumerate(blocks):
                        jj = kb - 4 * T
                        items.append(dict(T=T, i=i, kb=kb, c0=max(jj, 0) * 128, dg=(jj >= 0), first=(i == 0),
                                          last=(i == len(blocks) - 1)))
                G = len(items)
                per = -(-len(pending) // G) if pending else 0

                def emit_z(g):
                    it = items[g]

                    def zf(e, kb=it["kb"], c0=it["c0"], T=it["T"]):
                        for h in range(2):
                            ins = e.matmul(zv[:, h, c0:512], lhsT=kT[64 * h:64 * h + 64, kb * 128:(kb + 1) * 128],
                                           rhs=qT[64 * h:64 * h + 64, T * 512 + c0:(T + 1) * 512],
                                           start=True, stop=True)
                        return ins
                    sc.add("pe", zf, r=[("kT", p, it["kb"] // 4), ("qT", p, it["T"])], w=PK(0, 2))

                def emit_tail(g):
                    it = items[g]; c0 = it["c0"]; b = g % 2; b3 = g % 3
                    sc.add("act", lambda e: e.activation(out=gb_[:, :, c0:512], in_=cv[:, :, c0:512], func=AF.Exp),
                           r=PK(2, 2), w=["g"])
                    sc.add("dve", lambda e: e.tensor_tensor(out=abuf[b][:, :, c0:512], in0=eb[b3][:, :, c0:512],
                                                            in1=gb_[:, :, c0:512], op=ALU.mult),
                           r=["g", ("e", b3)], w=[("a", b)])

                def emit_av(g):
                    it = items[g]; c0 = it["c0"]; b = g % 2; kb = it["kb"]; first = it["first"]; T = it["T"]

                    def av(e):
                        for h in range(2):
                            ins = e.matmul(bank(4 + h)[:, c0:512], lhsT=vv[:, kb, :], rhs=abuf[b][:, h, c0:512],
                                           start=first, stop=False, skip_group_check=True)
                        return ins
                    sc.add("pe", av, r=[("a", b), ("vv", p, kb // 4)], w=PK(4, 2))
                    if it["last"]:
                        for h in range(2):
                            sc.add("dve", lambda e, h=h: e.tensor_copy(
                                out=oT[64 * h:64 * h + 64, T * 512:(T + 1) * 512], in_=bank(4 + h)[64 * h:64 * h + 64, :]),
                                r=PK(4 + h), w=[("oT", p, T)])

                def emit_fill():
                    def fill(e):
                        for _ in range(NFILL):
                            ins = e.matmul(bank(6), lhsT=negones, rhs=cst_b[:, 0:512], start=True, stop=True)
                        return ins
                    if NFILL > 0 and not pending:
                        sc.add("pe", fill, r=["cst_b"], w=PK(6))

                emit_z(0)
                for g in range(G):
                    it = items[g]; c0 = it["c0"]; b = g % 2; b3 = g % 3
                    if it["first"]:
                        sc.add("pool", lambda e: e.memset(R, 0.0), w=["R"])
                    sc.add("act", lambda e, c0=c0, b3=b3: e.activation(out=eb[b3][:, :, c0:512], in_=zv[:, :, c0:512],
                                                                       func=AF.Exp), r=PK(0, 2), w=[("e", b3)])
                    if it["dg"]:
                        mbc = mask_su.unsqueeze(1).to_broadcast([128, 2, 128])
                        sc.add("pool", lambda e, c0=c0, b3=b3, mbc=mbc: e.tensor_tensor(
                            out=eb[b3][:, :, c0:c0 + 128], in0=eb[b3][:, :, c0:c0 + 128], in1=mbc, op=ALU.mult),
                            r=[("e", b3), "cst_b"], w=[("e", b3)])
                    sc.add("act", lambda e, c0=c0, b=b, b3=b3: e.activation(out=spb[b][:, :, c0:512],
                                                                           in_=eb[b3][:, :, c0:512],
                                                                           func=AF.Ln, bias=1.0),
                           r=[("e", b3)], w=[("sp", b)])
                    if g + 1 < G:
                        emit_z(g + 1)
                    if g > 0:
                        emit_tail(g - 1)

                    def negc(e, c0=c0, b=b, first=it["first"]):
                        for h in range(2):
                            ins = e.matmul(cv[:, h, c0:512], lhsT=negtri, rhs=spb[b][:, h, c0:512], start=True,
                                           stop=first)
                            if not first:
                                ins = e.matmul(cv[:, h, c0:512], lhsT=negones, rhs=R[:, h, c0:512], start=False,
                                               stop=True)
                        return ins
                    sc.add("pe", negc, r=[("sp", b), "R", "cst_b"], w=PK(2, 2))
                    sc.add("dve", lambda e, c0=c0, b=b: e.tensor_tensor(out=R[:, :, c0:512], in0=R[:, :, c0:512],
                                                                       in1=spb[b][:, :, c0:512], op=ALU.add),
                           r=[("sp", b), "R"], w=["R"])
                    if g > 0:
                        emit_av(g - 1)
                    for _ in range(per):
                        if pending:
                            pending.pop(0)()
                    emit_fill()
                emit_tail(G - 1)
                emit_av(G - 1)
                while pending:
                    pending.pop(0)()

            if "noattn" not in dbg:
                load_w(0, 0); load_wo(0, 0)
                for stp in proj_steps(0, 0, True):
                    stp()
                for c in range(NCH):
                    p = c % 2
                    pending = []
                    if c + 1 < NCH:
                        load_w(c + 1, 1 - p)
                        pending += proj_steps(c + 1, 1 - p, False, fixed=6)
                    if c > 0:
                        opend = outproj_steps(c - 1, 1 - p, fixed=7)
                        mix = []
                        while pending or opend:
                            if pending: mix.append(pending.pop(0))
                            if pending: mix.append(pending.pop(0))
                            if opend: mix.append(opend.pop(0))
                        pending = mix
                    drain(2)
                    attention(c, p, pending)
                    if c + 1 < NCH:
                        load_wo(c + 1, 1 - p)
                for stp in outproj_steps(NCH - 1, (NCH - 1) % 2):
                    stp()
            sc.barrier()
            for T in range(NT):
                layer_norm(l, 0, T * 512, xbuf, xsqbuf, ([("lnxb", 0), ("lnxb", 1)], [("lnxsq", 0), ("lnxsq", 1)]),
                           (mean_sb, var_sb, rstd_sb))
            sc.barrier()

        for s in range(nseq):
            load_seq(s)
            for l in range(depth):
                if s == 0:
                    flush_until(2 * l)
                    cast_phase(2 * l + 2)
                if l % 2 == 0:
                    attn_layer(l)
                else:
                    gmlp_layer(l)
                if s == 0:
                    flush_until(2 * l + 1)
                    cast_phase(2 * l + 3)
                if "noffn" not in dbg:
                    ffn_layer(l)
            store_seq(s)
        sc.add("sp", None, r=[("yout", s, n) for s in range(nseq) for n in range(NTB)])
        sc.emit(nc, st)
    return nc, sc


def host_prep(inputs, depth):
    NG = depth // 2
    f = np.float32
    lnp = np.zeros((128, depth * 32), f)
    for l in range(depth):
        for which, (g, b) in enumerate(((inputs["ln_mix_g"], inputs["ln_mix_b"]), (inputs["ln_ffn_g"], inputs["ln_ffn_b"]))):
            lnp[:, l * 32 + which * 16: l * 32 + which * 16 + 8] = np.asarray(g[l], f).reshape(8, 128).T
            lnp[:, l * 32 + which * 16 + 8: l * 32 + which * 16 + 16] = np.asarray(b[l], f).reshape(8, 128).T
    convp = np.zeros((128, depth * 176), f)
    for l in range(depth):
        for k in range(3):
            convp[:, l * 176 + k * 44: l * 176 + (k + 1) * 44] = np.asarray(inputs["ffn_conv_w"][l, k], f).reshape(44, 128).T
        convp[:, l * 176 + 132: l * 176 + 176] = np.asarray(inputs["ffn_conv_b"][l], f).reshape(44, 128).T
    idx = np.arange(128)
    consts = np.zeros((128, 768), f)
    consts[:, 0:128] = np.eye(128, dtype=f)
    consts[:, 128:256] = -(idx[:, None] >= idx[None, :]).astype(f)
    consts[:, 256:384] = -1.0
    consts[:, 384:512] = (idx[:, None] < idx[None, :]).astype(f)
    consts[:, 512:640] = (idx[:, None] <= idx[None, :]).astype(f)
    consts[:, 640:768] = 1.0 / 1024.0
    ng = max(NG, 1)
    gb = np.zeros((ng, 2, 128, E2), f)
    bs = np.zeros((ng, 1, GG * 128), f)
    for j in range(NG):
        gb[j, 0] = np.broadcast_to(np.asarray(inputs["gmlp_ln_g"][j], f)[None, :], (128, E2))
        gb[j, 1] = np.broadcast_to(np.asarray(inputs["gmlp_ln_b"][j], f)[None, :], (128, E2))
        bs[j, 0] = np.asarray(inputs["gmlp_b_s"][j], f).reshape(-1)
    return {"lnp": lnp, "convp": convp, "consts": consts, "gb": gb, "bs": bs}


_CACHE = {}


def run(inputs, ncores, nseq, S, depth, ffn_t=1024, trace=False, dbg=()):
    key = (nseq, S, depth, ffn_t)
    if key not in _CACHE:
        _CACHE[key] = build(nseq, S, depth, ffn_t, dbg)
    nc, sc = _CACHE[key]
    NA = (depth + 1) // 2
    NG = max(depth // 2, 1)
    f = np.float32
    small = host_prep(inputs, depth)
    shared = {
        "attn_w_in": np.ascontiguousarray(inputs["attn_w_in"][:NA], f),
        "attn_w_out": np.ascontiguousarray(inputs["attn_w_out"][:NA], f),
        "gmlp_w_in": np.ascontiguousarray(inputs["gmlp_w_in"][:NG], f),
        "gmlp_w_out": np.ascontiguousarray(inputs["gmlp_w_out"][:NG], f),
        "ffn_w_up": np.ascontiguousarray(inputs["ffn_w_up"][:depth], f),
        "ffn_w_down": np.ascontiguousarray(inputs["ffn_w_down"][:depth], f),
        "gmlp_w_s": np.ascontiguousarray(inputs["gmlp_w_s"][:NG], f),
    }
    shared.update(small)
    x = np.asarray(inputs["x"], f)
    in_maps = []
    for i in range(ncores):
        m = dict(shared)
        m["x"] = np.ascontiguousarray(x[i * nseq:(i + 1) * nseq, :S])
        in_maps.append(m)
    res = run_bass_kernel_spmd(nc, in_maps, core_ids=list(range(ncores)), trace=trace)
    out = np.concatenate([np.asarray(r["y"]) for r in res.results], axis=0)
    return out.astype(np.float32), res


def kernel(**inputs):
    out, _ = run(inputs, 8, 4, 2048, 4)
    return out
```

```python
import contextlib
import numpy as np
import concourse.bass as bass
import concourse.mybir as mybir
from concourse.bass_utils import run_bass_kernel_spmd

F32 = mybir.dt.float32
BF16 = mybir.dt.bfloat16
AF = mybir.ActivationFunctionType
ALU = mybir.AluOpType
AX = mybir.AxisListType

D = 1024
NCH = 8
HEADS = 16
E2 = 2048
GG = 8
FF = 2816
NFF = 22
DEPTH_FULL = 4
ALPHA = (2 * DEPTH_FULL) ** 0.25
INV_ALPHA = 1.0 / ALPHA
EPS_RES = 1e-5 / (ALPHA * ALPHA)
EPS_G = 1e-5
ROT = 20000
NFILL = 2


class _Op:
    __slots__ = ("id", "eng", "fn", "deps", "dma", "ndma", "sig")


class Sched:
    ENGS = ("pe", "act", "dve", "pool", "sp")

    def __init__(self):
        self.ops = []
        self.lastw = {}
        self.readers = {}
        self.barrier_deps = set()
        self.since = {}
        self.dma_since = []

    def add(self, eng, fn, r=(), w=(), dma=None, ndma=1, nobar=False):
        o = _Op()
        o.id = len(self.ops); o.eng = eng; o.fn = fn; o.dma = dma; o.ndma = ndma; o.sig = None
        deps = set(self.barrier_deps)
        for k in r:
            if k in self.lastw:
                deps.add(self.lastw[k])
        for k in w:
            if k in self.lastw:
                deps.add(self.lastw[k])
            for rid in self.readers.get(k, {}).values():
                deps.add(rid)
        keep = set()
        for d in deps:
            p = self.ops[d]
            if p.dma is None and dma is None and p.eng == eng:
                if eng == "pe":
                    continue
                raw = any(self.lastw.get(k) == d for k in r) or any(self.lastw.get(k) == d for k in w)
                if eng != "pool" and not raw and d not in self.barrier_deps:
                    continue
            keep.add(d)
        o.deps = keep
        for k in r:
            rk = self.readers.setdefault(k, {})
            rk[eng if dma is None else ("dma", o.id)] = o.id
        for k in w:
            self.lastw[k] = o.id
            self.readers[k] = {}
        self.ops.append(o)
        if dma is None:
            self.since[eng] = o.id
        elif not nobar:
            self.dma_since.append(o.id)
        return o.id

    def barrier(self):
        self.barrier_deps = set(self.since.values()) | set(self.dma_since)
        self.dma_since = []

    def emit(self, nc, st):
        ops = self.ops
        has_dep = [False] * len(ops)
        for o in ops:
            for d in o.deps:
                has_dep[d] = True
        cnt = {e: 0 for e in self.ENGS}
        dcnt = {}
        for o in ops:
            if o.dma is None:
                if has_dep[o.id]:
                    c = cnt[o.eng]; cnt[o.eng] = c + 1
                    o.sig = (("E", o.eng, c // ROT), c % ROT + 1)
            else:
                c = dcnt.get(o.dma, 0) + o.ndma; dcnt[o.dma] = c
                o.sig = (("D", o.dma), 16 * c)
        sems = {}
        for e in self.ENGS:
            for k in range((cnt[e] + ROT - 1) // ROT):
                sems[("E", e, k)] = st.enter_context(nc.semaphore(f"s_{e}{k}"))
        for i, k in enumerate(sorted(dcnt, key=str)):
            sems[("D", k)] = st.enter_context(nc.semaphore(f"d{i}"))
        self.nsems = len(sems)
        byeng = {e: [] for e in self.ENGS}
        for o in ops:
            byeng[o.eng].append(o)

        def run(ename, eng):
            waited = {}
            for o in byeng[ename]:
                need = {}
                for d in o.deps:
                    s, v = ops[d].sig
                    if need.get(s, 0) < v:
                        need[s] = v
                for s, v in need.items():
                    if waited.get(s, 0) < v:
                        eng.wait_ge(sems[s], v)
                        waited[s] = v
                if o.fn is None:
                    continue
                ins = o.fn(eng)
                if o.dma is not None:
                    if not isinstance(ins, (list, tuple)):
                        ins = [ins]
                    assert len(ins) == o.ndma, (len(ins), o.ndma)
                    for x in ins:
                        x.then_inc(sems[("D", o.dma)], 16)
                elif has_dep[o.id]:
                    ins.then_inc(sems[o.sig[0]], 1)

        block = st.enter_context(nc.Block())

        @block.tensor
        def _(e):
            run("pe", e)

        @block.scalar
        def _(e):
            run("act", e)

        @block.vector
        def _(e):
            run("dve", e)

        @block.gpsimd
        def _(e):
            run("pool", e)

        @block.sync
        def _(e):
            run("sp", e)


def build(nseq, S, depth, ffn_t=1024, dbg=()):
    nc = bass.Bass("TRN2", target_bir_lowering=False)
    NA = (depth + 1) // 2
    NG = depth // 2
    NTB = S // 128
    TT = 512
    NT = S // TT
    FT = min(ffn_t, S)
    NFT = S // FT

    def din(name, shape, dt=F32):
        return nc.dram_tensor(name, list(shape), dt, kind="ExternalInput").ap()

    x_d = din("x", [nseq, S, D])
    y_d = nc.dram_tensor("y", [nseq, S, D], F32, kind="ExternalOutput").ap()
    w_ain = din("attn_w_in", [NA, D, 3 * D])
    w_aout = din("attn_w_out", [NA, D, D])
    w_gin = din("gmlp_w_in", [max(NG, 1), D, 2 * E2])
    w_gout = din("gmlp_w_out", [max(NG, 1), E2, D])
    w_up = din("ffn_w_up", [depth, D, 2 * FF])
    w_dn = din("ffn_w_down", [depth, FF, D])
    ws_d = din("gmlp_w_s", [max(NG, 1), GG, 128, 128])
    bs_d = din("bs", [max(NG, 1), 1, GG * 128])
    gb_d = din("gb", [max(NG, 1), 2, 128, E2])
    lnp_d = din("lnp", [128, depth * 32])
    conv_d = din("convp", [128, depth * 4 * 44])
    cst_d = din("consts", [128, 6 * 128])

    def dscr(name, shape):
        return nc.dram_tensor(name, list(shape), BF16, kind="Internal").ap()

    b_ain = dscr("b_ain", [NA, D, 3 * D])
    b_aout = dscr("b_aout", [NA, D, D])
    b_gin = dscr("b_gin", [max(NG, 1), D, 2 * E2])
    b_gout = dscr("b_gout", [max(NG, 1), E2, D])
    b_up = dscr("b_up", [depth, D, 2 * FF])
    b_dn = dscr("b_dn", [depth, FF, D])

    sc = Sched()
    st = contextlib.ExitStack()
    with st:
        E = st.enter_context
        h32 = E(nc.sbuf_tensor("h32", [128, NCH, S], F32))
        AB_N = 43 * 1024
        AF_N = 11 * 1024
        arena_b = E(nc.sbuf_tensor("arena_b", [128, AB_N], BF16))
        arena_f = E(nc.sbuf_tensor("arena_f", [128, AF_N], F32))
        cst_f = E(nc.sbuf_tensor("cst_f", [128, 6 * 128], F32))
        cst_b = E(nc.sbuf_tensor("cst_b", [128, 6 * 128], BF16))
        lnp = E(nc.sbuf_tensor("lnp_sb", [128, depth * 32], F32))
        convp = E(nc.sbuf_tensor("convp_sb", [128, depth * 176], F32))
        wct = E(nc.sbuf_tensor("wct", [128, max(NG, 1) * GG * 128], BF16))
        ps = E(nc.psum_tensor("ps", [128, 4096], F32))

        ident = cst_f[:, 0:128]
        negtri = cst_b[:, 128:256]
        negones = cst_b[:, 256:384]
        mask_su = cst_b[:, 384:512]
        mask_ui_f = cst_f[:, 512:640]
        onesd = cst_b[:, 640:768]

        class Arena:
            def __init__(self, t, n, esz):
                self.t = t; self.n = n; self.off = 0; self.esz = esz

            def reset(self):
                self.off = 0

            def alloc(self, *dims):
                n = int(np.prod(dims))
                a = self.off
                self.off = a + ((n + 31) // 32) * 32
                assert self.off <= self.n, ("arena overflow", self.off, self.n)
                v = self.t[:, a:a + n]
                self.last_flat = v
                if len(dims) == 2:
                    v = v.rearrange("p (a b) -> p a b", a=dims[0])
                elif len(dims) == 3:
                    v = v.rearrange("p (a b c) -> p a b c", a=dims[0], b=dims[1])
                return v

        ab = Arena(arena_b, AB_N, 2)
        af = Arena(arena_f, AF_N, 4)

        def bank(i, n=1):
            return ps[:, i * 512:(i + n) * 512]

        def PK(i, n=1):
            return [("ps", i + k) for k in range(n)]

        sc.add("sp", lambda e: e.dma_start(out=cst_f[:, :], in_=cst_d[:, :]), w=["cst_f"], dma="cst")
        sc.add("sp", lambda e: e.dma_start(out=lnp[:, :], in_=lnp_d[:, :]), w=["lnp"], dma="lnp")
        sc.add("sp", lambda e: e.dma_start(out=convp[:, :], in_=conv_d[:, :]), w=["convp"], dma="convp")
        sc.add("dve", lambda e: e.tensor_copy(out=cst_b[:, :], in_=cst_f[:, :]), r=["cst_f"], w=["cst_b"])

        cast_q = []
        castkeys = {}
        cur_ph = [0]

        def cast_all(dst, src, key, rows, step=256):
            keys = []
            for r0 in range(0, rows, step):
                r1 = min(rows, r0 + step)
                k = key + (r0,)
                keys.append(k)

                def emit(d=dst, s_=src, a=r0, b=r1, k=k, key=key):
                    sc.add("pool", lambda e: e.dma_start(out=d[a:b, :], in_=s_[a:b, :]),
                           w=[k], dma="wc_%s_%d" % key, nobar=True)
                cast_q.append((cur_ph[0], emit))
            castkeys[key] = keys

        def drain(n):
            for _ in range(n):
                if cast_q:
                    cast_q.pop(0)[1]()

        def flush_until(ph):
            while cast_q and cast_q[0][0] <= ph:
                cast_q.pop(0)[1]()

        def cast_phase(ph):
            if ph >= 2 * depth:
                return
            cur_ph[0] = ph
            l = ph // 2; j = l // 2
            if ph % 2 == 0:
                if l % 2 == 0:
                    cast_all(b_ain[j], w_ain[j], ("c_ain", j), D)
                    cast_all(b_aout[j], w_aout[j], ("c_aout", j), D)
                else:
                    cast_all(b_gin[j], w_gin[j], ("c_gin", j), D)
                    cast_all(b_gout[j], w_gout[j], ("c_gout", j), E2)
            else:
                cast_all(b_up[l], w_up[l], ("c_up", l), D)
                cast_all(b_dn[l], w_dn[l], ("c_dn", l), FF)

        cast_phase(0)
        cast_phase(1)
        flush_until(1)

        ab.reset(); af.reset()
        if NG > 0:
            wst = arena_f[:, AF_N - 256:AF_N].rearrange("p (a b) -> p a b", a=2)
            for j in range(NG):
                for g in range(GG):
                    b = (j * GG + g) % 2
                    sc.add("sp", lambda e, j=j, g=g, b=b: e.dma_start(out=wst[:, b, :], in_=ws_d[j, g, :, :]),
                           w=[("wst", b)], dma=f"wst{b}")
                    sc.add("pe", lambda e, b=b: e.transpose(out=bank(b)[:, 0:128], in_=wst[:, b, :], identity=ident),
                           r=[("wst", b), "cst_f"], w=PK(b))
                    sc.add("dve", lambda e, j=j, g=g, b=b: e.tensor_tensor(
                        out=wct[:, (j * GG + g) * 128:(j * GG + g + 1) * 128], in0=bank(b)[:, 0:128],
                        in1=mask_ui_f, op=ALU.mult), r=PK(b) + ["cst_f"], w=["wct"])

        def hkey(c, t0, t1):
            return [("h", c, k) for k in range(t0 // 512, (t1 + 511) // 512)]

        def hkeys_all(t0, t1):
            out = []
            for c in range(NCH):
                out += hkey(c, t0, t1)
            return out

        def layer_norm(li, which, t0, xb, xsq, kxb, stats):
            mean_sb, var_sb, rstd_sb = stats
            W = 512
            g_off = li * 32 + which * 16
            for hf in range(2):
                cs = slice(4 * hf, 4 * hf + 4)
                hk = []
                for c in range(4 * hf, 4 * hf + 4):
                    hk += hkey(c, t0, t0 + W)
                sc.add("dve", lambda e, cs=cs: e.tensor_copy(out=xb[:, cs, :], in_=h32[:, cs, t0:t0 + W]),
                       r=hk, w=[kxb[0][hf]])
                sc.add("act", lambda e, cs=cs: e.activation(out=xsq[:, cs, :], in_=h32[:, cs, t0:t0 + W],
                                                            func=AF.Square), r=hk, w=[kxb[1][hf]])
            for hf in range(2):
                def mm_stats(e, hf=hf, dst=bank(6), src=xb):
                    for c in range(4 * hf, 4 * hf + 4):
                        i = e.matmul(dst, lhsT=onesd, rhs=src[:, c, :], start=(c == 0), stop=(c == NCH - 1))
                    return i
                sc.add("pe", mm_stats, r=[kxb[0][hf], "cst_b"], w=PK(6))
            for hf in range(2):
                def mm_stats2(e, hf=hf, dst=bank(7), src=xsq):
                    for c in range(4 * hf, 4 * hf + 4):
                        i = e.matmul(dst, lhsT=onesd, rhs=src[:, c, :], start=(c == 0), stop=(c == NCH - 1))
                    return i
                sc.add("pe", mm_stats2, r=[kxb[1][hf], "cst_b"], w=PK(7))
            sc.add("act", lambda e: e.activation(out=mean_sb, in_=bank(6), func=AF.Copy), r=PK(6), w=["ln_mean"])
            sc.add("dve", lambda e: e.tensor_tensor(out=var_sb, in0=mean_sb, in1=mean_sb, op=ALU.mult),
                   r=["ln_mean"], w=["ln_var"])
            sc.add("dve", lambda e: e.scalar_tensor_tensor(out=var_sb, in0=bank(7), scalar=EPS_RES, in1=var_sb,
                                                           op0=ALU.add, op1=ALU.subtract),
                   r=PK(7) + ["ln_var"], w=["ln_var"])
            sc.add("act", lambda e: e.activation(out=var_sb, in_=var_sb, func=AF.Ln), r=["ln_var"], w=["ln_var"])
            sc.add("act", lambda e: e.activation(out=rstd_sb, in_=var_sb, func=AF.Exp, scale=-0.5),
                   r=["ln_var"], w=["ln_rstd"])
            order = [7, 0, 1, 2, 3, 4, 5, 6]
            for c in order:
                eng = "pool" if c >= 7 else "dve"
                hc = h32[:, c, t0:t0 + W]
                kc = hkey(c, t0, t0 + W)
                sc.add(eng, lambda e, hc=hc: e.tensor_tensor(out=hc, in0=hc, in1=mean_sb, op=ALU.subtract),
                       r=kc + ["ln_mean"], w=kc)
                sc.add(eng, lambda e, hc=hc: e.tensor_tensor(out=hc, in0=hc, in1=rstd_sb, op=ALU.mult),
                       r=kc + ["ln_rstd"], w=kc)
            for c in order:
                sc.add("act", lambda e, c=c: e.activation(
                    out=h32[:, c, t0:t0 + W], in_=h32[:, c, t0:t0 + W], func=AF.Identity,
                    scale=lnp[:, g_off + c:g_off + c + 1], bias=lnp[:, g_off + 8 + c:g_off + 9 + c]),
                    r=hkey(c, t0, t0 + W) + ["lnp"], w=hkey(c, t0, t0 + W))

        def resid_add(psrc, pkeys, c, t0, w):
            sc.add("dve", lambda e: e.scalar_tensor_tensor(
                out=h32[:, c, t0:t0 + w], in0=psrc, scalar=INV_ALPHA, in1=h32[:, c, t0:t0 + w],
                op0=ALU.mult, op1=ALU.add), r=pkeys + hkey(c, t0, t0 + w), w=hkey(c, t0, t0 + w))

        def load_seq(s):
            ab.reset(); af.reset()
            xs = af.alloc(2, D)
            for n in range(NTB):
                b = n % 2
                sc.add("sp", lambda e, n=n, b=b: e.dma_start(out=xs[:, b, :], in_=x_d[s, n * 128:(n + 1) * 128, :]),
                       w=[("xs", b)], dma=f"xs{b}")

                def tr(e, n=n, b=b):
                    for c in range(NCH):
                        i = e.transpose(out=ps[:, b * 1024 + c * 128: b * 1024 + (c + 1) * 128],
                                        in_=xs[:, b, c * 128:(c + 1) * 128], identity=ident)
                    return i
                sc.add("pe", tr, r=[("xs", b), "cst_f"], w=PK(2 * b, 2))
                src = ps[:, b * 1024:(b + 1) * 1024].rearrange("p (c t) -> p c t", c=NCH)
                eng = "act" if n % 2 == 0 else "dve"
                if eng == "act":
                    sc.add("act", lambda e, n=n, src=src: e.activation(out=h32[:, :, n * 128:(n + 1) * 128], in_=src,
                                                                       func=AF.Copy),
                           r=PK(2 * b, 2), w=hkeys_all(n * 128, (n + 1) * 128))
                else:
                    sc.add("dve", lambda e, n=n, src=src: e.tensor_copy(out=h32[:, :, n * 128:(n + 1) * 128], in_=src),
                           r=PK(2 * b, 2), w=hkeys_all(n * 128, (n + 1) * 128))
            sc.barrier()

        def store_seq(s):
            ab.reset(); af.reset()
            ys = af.alloc(2, D)
            for n in range(NTB):
                b = n % 2

                def tr(e, n=n, b=b):
                    for c in range(NCH):
                        i = e.transpose(out=ps[:, b * 1024 + c * 128: b * 1024 + (c + 1) * 128],
                                        in_=h32[:, c, n * 128:(n + 1) * 128], identity=ident)
                    return i
                sc.add("pe", tr, r=hkeys_all(n * 128, (n + 1) * 128) + ["cst_f"], w=PK(2 * b, 2))
                if n % 2 == 0:
                    sc.add("act", lambda e, b=b: e.activation(out=ys[:, b, :], in_=ps[:, b * 1024:(b + 1) * 1024],
                                                              func=AF.Copy), r=PK(2 * b, 2), w=[("ys", b)])
                else:
                    sc.add("dve", lambda e, b=b: e.tensor_copy(out=ys[:, b, :], in_=ps[:, b * 1024:(b + 1) * 1024]),
                           r=PK(2 * b, 2), w=[("ys", b)])
                sc.add("sp", lambda e, n=n, b=b: e.dma_start(out=y_d[s, n * 128:(n + 1) * 128, :], in_=ys[:, b, :]),
                       r=[("ys", b)], w=[("yout", s, n)], dma=f"yout{b}")
            sc.barrier()

        def ffn_layer(l):
            ab.reset(); af.reset()
            hbT = ab.alloc(NCH, FT)
            pT = ab.alloc(11, FT)
            wup = [ab.alloc(2, NCH, 128) for _ in range(3)]
            wdn = [ab.alloc(11, 128) for _ in range(2)]
            raw = [[af.alloc(FT + 2) for _ in range(2)] for _ in range(2)]
            yy = [[af.alloc(FT) for _ in range(2)] for _ in range(2)]
            hal = af.alloc(44, 2)
            mean_sb = af.alloc(512); var_sb = af.alloc(512); rstd_sb = af.alloc(512)
            xbuf = ab.alloc(NCH, 512); xsqbuf = ab.alloc(NCH, 512)
            cp = lambda k, i: convp[:, l * 176 + k * 44 + i: l * 176 + k * 44 + i + 1]
            wupv = b_up[l].rearrange("(c p) n -> p c n", p=128)
            wdnv = b_dn[l].rearrange("(k p) n -> p k n", p=128)
            nup = 0
            ndnc = [0]
            sc.add("pool", lambda e: e.memset(hal, 0.0), w=["hal"])
            def cast_hbT(ft):
                t0 = ft * FT
                for hf in range(2):
                    cs = slice(4 * hf, 4 * hf + 4)
                    hk = []
                    for c in range(4 * hf, 4 * hf + 4):
                        hk += hkey(c, t0, t0 + FT)
                    if hf == 0:
                        sc.add("act", lambda e, cs=cs: e.activation(out=hbT[:, cs, :], in_=h32[:, cs, t0:t0 + FT],
                                                                    func=AF.Copy), r=hk, w=[("hbT", hf)])
                    else:
                        sc.add("dve", lambda e, cs=cs: e.tensor_copy(out=hbT[:, cs, :], in_=h32[:, cs, t0:t0 + FT]),
                               r=hk, w=[("hbT", hf)])
            cast_hbT(0)
            tail = [None]
            ln_pending = [None]
            dn_pending = [None]
            nupt = [0]
            for ft in range(NFT):
                t0 = ft * FT
                for half in range(2):
                    for il in range(11):
                        i = half * 11 + il
                        slot = nup % 3; nup += 1
                        wt = wup[slot]
                        sc.add("sp", lambda e, wt=wt, i=i: [
                            e.dma_start(out=wt[:, 0, :, :], in_=wupv[:, :, i * 128:(i + 1) * 128]),
                            e.dma_start(out=wt[:, 1, :, :], in_=wupv[:, :, FF + i * 128:FF + (i + 1) * 128])],
                            r=castkeys[("c_up", l)], w=[("wup", slot)], dma=f"wup{slot}", ndma=2)
                        for gv in range(2):
                            bk = (nupt[0] % 3) * 2; nupt[0] += 1
                            idx = gv * NFF + i
                            rw = raw[gv][il % 2]
                            y = yy[gv][il % 2]
                            kr = ("raw", gv, il % 2); krh = ("rawh", gv, il % 2); ky = ("yy", gv, il % 2)

                            def up(e, wt=wt, gv=gv, bk=bk):
                                for c in range(NCH):
                                    for tt in range(FT // 512):
                                        ins = e.matmul(bank(bk + tt), lhsT=wt[:, gv, c, :],
                                                       rhs=hbT[:, c, tt * 512:(tt + 1) * 512],
                                                       start=(c == 0), stop=(c == NCH - 1))
                                return ins
                            sc.add("pe", up, r=[("wup", slot), ("hbT", 0), ("hbT", 1)], w=PK(bk, FT // 512))
                            psrc = ps[:, bk * 512: bk * 512 + FT]
                            sc.add("pool", lambda e, rw=rw, idx=idx: e.tensor_copy(out=rw[:, 0:2], in_=hal[:, idx, :]),
                                   r=["hal"], w=[krh])
                            sc.add("act", lambda e, rw=rw, psrc=psrc: e.activation(out=rw[:, 2:FT + 2], in_=psrc,
                                                                                    func=AF.Copy),
                                   r=PK(bk, FT // 512), w=[kr])
                            sc.add("act", lambda e, y=y, psrc=psrc, idx=idx: e.activation(
                                out=y, in_=psrc, func=AF.Identity, scale=cp(2, idx), bias=cp(3, idx)),
                                r=PK(bk, FT // 512) + ["convp"], w=[ky])
                            sc.add("dve", lambda e, y=y, rw=rw, idx=idx: e.scalar_tensor_tensor(
                                out=y, in0=rw[:, 1:FT + 1], scalar=cp(1, idx), in1=y, op0=ALU.mult, op1=ALU.add),
                                r=[kr, krh, ky, "convp"], w=[ky])
                            sc.add("dve", lambda e, y=y, rw=rw, idx=idx: e.scalar_tensor_tensor(
                                out=y, in0=rw[:, 0:FT], scalar=cp(0, idx), in1=y, op0=ALU.mult, op1=ALU.add),
                                r=[kr, krh, ky, "convp"], w=[ky])
                            sc.add("pool", lambda e, rw=rw, idx=idx: e.tensor_copy(out=hal[:, idx, :],
                                                                                  in_=rw[:, FT:FT + 2]),
                                   r=[kr], w=["hal"])
                        if il % 3 == 0:
                            drain(2)
                        if dn_pending[0] is not None and half == 1 and il == 0:
                            dn_pending[0](); dn_pending[0] = None
                        if tail[0] is not None:
                            tail[0](); tail[0] = None

                        def mk_tail(il=il):
                            yg = yy[0][il % 2]; yv = yy[1][il % 2]

                            def t():
                                sc.add("act", lambda e: e.activation(out=yg, in_=yg, func=AF.Silu),
                                       r=[("yy", 0, il % 2)], w=[("yy", 0, il % 2)])
                                sc.add("dve", lambda e: e.tensor_tensor(out=pT[:, il, :], in0=yg, in1=yv, op=ALU.mult),
                                       r=[("yy", 0, il % 2), ("yy", 1, il % 2)], w=[("pT", il)])
                            return t
                        tail[0] = mk_tail()
                        if ln_pending[0] is not None and half == 0 and il == 0:
                            ln_pending[0](); ln_pending[0] = None
                    if tail[0] is not None:
                        tail[0](); tail[0] = None
                    if half == 1 and ft + 1 < NFT:
                        cast_hbT(ft + 1)

                    def emit_dn(half=half, t0=t0):
                        for fo in range(NCH):
                            slot = ndnc[0] % 2; ndnc[0] += 1
                            wd = wdn[slot]
                            sc.add("sp", lambda e, wd=wd, fo=fo: e.dma_start(
                                out=wd, in_=wdnv[:, half * 11:(half + 1) * 11, fo * 128:(fo + 1) * 128]),
                                r=castkeys[("c_dn", l)], w=[("wdn", slot)], dma=f"wdn{slot}")
                            for tt in range(FT // 512):
                                bk = 6 + tt

                                def dn(e, wd=wd, tt=tt, bk=bk):
                                    for k in range(11):
                                        ins = e.matmul(bank(bk), lhsT=wd[:, k, :], rhs=pT[:, k, tt * 512:(tt + 1) * 512],
                                                       start=(k == 0), stop=(k == 10))
                                    return ins
                                sc.add("pe", dn, r=[("wdn", slot)] + [("pT", k) for k in range(11)], w=PK(bk))
                                resid_add(bank(bk), PK(bk), fo, t0 + tt * 512, 512)
                    if half == 0:
                        dn_pending[0] = emit_dn
                    else:
                        emit_dn()
                def do_ln(t0=t0):
                    for tt in range(FT // 512):
                        layer_norm(l, 1, t0 + tt * 512, xbuf, xsqbuf,
                                   ([("lnxb", 0), ("lnxb", 1)], [("lnxsq", 0), ("lnxsq", 1)]), (mean_sb, var_sb, rstd_sb))
                if ft + 1 < NFT:
                    ln_pending[0] = do_ln
                else:
                    do_ln()
            sc.barrier()

        def gmlp_layer(l):
            j = l // 2
            ab.reset(); af.reset()
            hbT = ab.alloc(NCH, TT)
            wv = ab.alloc(NCH, E2)
            uT = ab.alloc(16, TT)
            vln = ab.alloc(4, E2)
            vflat = ab.last_flat
            xbuf = vflat[:, 0:NCH * 512].rearrange("p (c t) -> p c t", c=NCH)
            xsqbuf = vflat[:, NCH * 512:2 * NCH * 512].rearrange("p (c t) -> p c t", c=NCH)
            wu = [ab.alloc(NCH, 128) for _ in range(2)]
            wo = [ab.alloc(16, 128) for _ in range(2)]
            bsb = ab.alloc(GG * 128)
            v32 = [af.alloc(E2) for _ in range(2)]
            Gt = af.alloc(E2); Bt = af.alloc(E2)
            bsf = af.alloc(GG * 128)
            bst = af.alloc(4 * 6); mv = af.alloc(2); rs = af.alloc(1)
            mean_sb = af.alloc(512); var_sb = af.alloc(512); rstd_sb = af.alloc(512)
            winv = b_gin[j].rearrange("(c p) n -> p c n", p=128)
            woutv = b_gout[j].rearrange("(k p) n -> p k n", p=128)
            ck_in = castkeys[("c_gin", j)]; ck_out = castkeys[("c_gout", j)]
            for q in range(4):
                sc.add("sp", lambda e, q=q: e.dma_start(out=wv[:, :, q * 512:(q + 1) * 512],
                                                        in_=winv[:, :, E2 + q * 512:E2 + (q + 1) * 512]),
                       r=ck_in, w=[("wv", q)], dma=f"wv{q}")
            sc.add("sp", lambda e: e.dma_start(out=Gt, in_=gb_d[j, 0, :, :]), w=["Gt"], dma="Gt")
            sc.add("sp", lambda e: e.dma_start(out=Bt, in_=gb_d[j, 1, :, :]), w=["Bt"], dma="Bt")
            sc.add("sp", lambda e: e.dma_start(out=bsf[0:1, :], in_=bs_d[j, :, :]), w=["bsf"], dma="bsf")
            sc.add("dve", lambda e: e.tensor_copy(out=bsb[0:1, :], in_=bsf[0:1, :]), r=["bsf"], w=["bsb"])
            nwu = 0; nwo = 0; nps = 0
            def cast_hbT(T):
                tq = T * TT
                sc.add("dve", lambda e: e.tensor_copy(out=hbT, in_=h32[:, :, tq:tq + TT]),
                       r=hkeys_all(tq, tq + TT), w=["hbT"])
            cast_hbT(0)
            for T in range(NT):
                t0 = T * TT
                def u_steps(fos):
                  nonlocal nwu, nps
                  for fo in fos:
                      slot = nwu % 2; nwu += 1
                      wt = wu[slot]
                      sc.add("sp", lambda e, wt=wt, fo=fo: e.dma_start(out=wt, in_=winv[:, :, fo * 128:(fo + 1) * 128]),
                             r=ck_in, w=[("wu", slot)], dma=f"wu{slot}")
                      bk = 4 + nps % 2; nps += 1

                      def uproj(e, wt=wt, bk=bk):
                          for c in range(NCH):
                              ins = e.matmul(bank(bk), lhsT=wt[:, c, :], rhs=hbT[:, c, :], start=(c == 0),
                                             stop=(c == NCH - 1))
                          return ins
                      sc.add("pe", uproj, r=[("wu", slot), "hbT"], w=PK(bk))
                      sc.add("act", lambda e, fo=fo, bk=bk: e.activation(out=uT[:, fo, :], in_=bank(bk),
                                                                         func=AF.Gelu_apprx_tanh),
                             r=PK(bk), w=[("uT", fo)])
                for n in range(4):
                    vb = v32[n % 2]; kv = ("v32", n % 2)
                    for hf in range(2):
                        def vproj(e, n=n, hf=hf):
                            for q in range(2 * hf, 2 * hf + 2):
                                for c in range(NCH):
                                    ins = e.matmul(bank(q), lhsT=hbT[:, c, n * 128:(n + 1) * 128],
                                                   rhs=wv[:, c, q * 512:(q + 1) * 512], start=(c == 0),
                                                   stop=(c == NCH - 1))
                            return ins
                        sc.add("pe", vproj, r=["hbT", ("wv", 2 * hf), ("wv", 2 * hf + 1)], w=PK(2 * hf, 2))
                        sc.add("act", lambda e, vb=vb, hf=hf: e.activation(
                            out=vb[:, hf * 1024:(hf + 1) * 1024], in_=ps[:, hf * 1024:(hf + 1) * 1024],
                            func=AF.Gelu_apprx_tanh), r=PK(2 * hf, 2), w=[kv + (hf,)])
                        for q in range(2 * hf, 2 * hf + 2):
                            sc.add("dve", lambda e, vb=vb, q=q: e.bn_stats(out=bst[:, q * 6:(q + 1) * 6],
                                                                           in_=vb[:, q * 512:(q + 1) * 512]),
                                   r=[kv + (hf,)], w=[("bst", q)])
                        u_steps(range(4 * n + 2 * hf, 4 * n + 2 * hf + 2))
                    kvs = [kv + (0,), kv + (1,)]
                    sc.add("dve", lambda e: e.bn_aggr(out=mv, in_=bst), r=[("bst", q) for q in range(4)], w=["mv"])
                    sc.add("dve", lambda e: e.tensor_scalar(out=rs, in0=mv[:, 1:2], scalar1=EPS_G, scalar2=None,
                                                            op0=ALU.add), r=["mv"], w=["rs"])
                    sc.add("act", lambda e: e.activation(out=rs, in_=rs, func=AF.Ln), r=["rs"], w=["rs"])
                    sc.add("act", lambda e: e.activation(out=rs, in_=rs, func=AF.Exp, scale=-0.5), r=["rs"], w=["rs"])
                    sc.add("dve", lambda e, vb=vb: e.scalar_tensor_tensor(out=vb, in0=vb, scalar=mv[:, 0:1], in1=Gt,
                                                                          op0=ALU.subtract, op1=ALU.mult),
                           r=kvs + ["mv", "Gt"], w=kvs)
                    sc.add("dve", lambda e, vb=vb, n=n: e.scalar_tensor_tensor(out=vln[:, n, :], in0=vb, scalar=rs,
                                                                               in1=Bt, op0=ALU.mult, op1=ALU.add),
                           r=kvs + ["rs", "Bt"], w=[("vln", n)])
                    drain(1)
                drain(4)
                if T + 1 < NT:
                    cast_hbT(T + 1)
                for cj in range(16):
                    g = cj // 2
                    bk = 4 + nps % 2; nps += 1

                    def spat(e, cj=cj, g=g, bk=bk):
                        brow = bsb[0:1, g * 128:(g + 1) * 128].unsqueeze(1).to_broadcast([1, 4, 128])
                        e.matmul(bank(bk).rearrange("p (n t) -> p n t", n=4), lhsT=cst_b[0:1, 512:640], rhs=brow,
                                 start=True, stop=False, skip_group_check=True)
                        for n in range(4):
                            ins = e.matmul(bank(bk)[:, n * 128:(n + 1) * 128], lhsT=vln[:, n, cj * 128:(cj + 1) * 128],
                                           rhs=wct[:, (j * GG + g) * 128:(j * GG + g + 1) * 128], start=False,
                                           stop=(n == 3), skip_group_check=True)
                        return ins
                    sc.add("pe", spat, r=[("vln", n) for n in range(4)] + ["bsb", "wct", "cst_b"], w=PK(bk))
                    sc.add("dve", lambda e, cj=cj, bk=bk: e.tensor_tensor(out=uT[:, cj, :], in0=bank(bk),
                                                                          in1=uT[:, cj, :], op=ALU.mult),
                           r=PK(bk) + [("uT", cj)], w=[("uT", cj)])
                for fo in range(NCH):
                    slot = nwo % 2; nwo += 1
                    wt = wo[slot]
                    sc.add("sp", lambda e, wt=wt, fo=fo: e.dma_start(out=wt, in_=woutv[:, :, fo * 128:(fo + 1) * 128]),
                           r=ck_out, w=[("wo", slot)], dma=f"wo{slot}")
                    bk = 4 + nps % 2; nps += 1

                    def oproj(e, wt=wt, bk=bk):
                        for k in range(16):
                            ins = e.matmul(bank(bk), lhsT=wt[:, k, :], rhs=uT[:, k, :], start=(k == 0), stop=(k == 15))
                        return ins
                    sc.add("pe", oproj, r=[("wo", slot)] + [("uT", k) for k in range(16)], w=PK(bk))
                    resid_add(bank(bk), PK(bk), fo, t0, 512)
                layer_norm(l, 0, t0, xbuf, xsqbuf, ([("vln", 0), ("vln", 1)], [("vln", 2), ("vln", 3)]),
                           (mean_sb, var_sb, rstd_sb))
            sc.barrier()

        def attn_layer(l):
            j = l // 2
            ab.reset(); af.reset()
            hb = ab.alloc(NCH, S)
            QK = [(ab.alloc(S), ab.alloc(S), ab.alloc(NTB, 128)) for _ in range(2)]
            oTs = [ab.alloc(S) for _ in range(2)]
            scr = ab.alloc(18, 512)
            eb = [scr[:, 0:2, :], scr[:, 2:4, :], scr[:, 16:18, :]]
            spb = [scr[:, 4:6, :], scr[:, 6:8, :]]
            gb_ = scr[:, 8:10, :]
            abuf = [scr[:, 10:12, :], scr[:, 12:14, :]]
            R = scr[:, 14:16, :]
            xbuf = scr[:, 0:8, :]; xsqbuf = scr[:, 8:16, :]
            mean_sb = af.alloc(512); var_sb = af.alloc(512); rstd_sb = af.alloc(512)
            wreg = arena_f[:, 2048:2048 + 4096].bitcast(BF16)
            Ws = []
            for p in range(2):
                base = p * 4096
                Ws.append((wreg[:, base:base + 1024].rearrange("p (a b) -> p a b", a=NCH),
                           wreg[:, base + 1024:base + 2048].rearrange("p (a b) -> p a b", a=NCH),
                           wreg[:, base + 2048:base + 3072].rearrange("p (a b) -> p a b", a=NCH),
                           wreg[:, base + 3072:base + 4096]))
            winv = b_ain[j].rearrange("(c p) n -> p c n", p=128)
            ck_in = castkeys[("c_ain", j)]; ck_out = castkeys[("c_aout", j)]
            for c in range(NCH):
                sc.add("act" if c % 2 == 0 else "dve",
                       (lambda e, c=c: e.activation(out=hb[:, c, :], in_=h32[:, c, :], func=AF.Copy)) if c % 2 == 0 else
                       (lambda e, c=c: e.tensor_copy(out=hb[:, c, :], in_=h32[:, c, :])),
                       r=hkey(c, 0, S), w=[("hb", c)])
            hbk = [("hb", c) for c in range(NCH)]
            zv = ps[:, 0:1024].rearrange("p (h n) -> p h n", h=2)
            cv = ps[:, 1024:2048].rearrange("p (h n) -> p h n", h=2)
            cnt = {"npj": 0}

            def nbank(fixed):
                if fixed is not None:
                    return fixed
                bk = 6 + cnt["npj"] % 2; cnt["npj"] += 1
                return bk

            def load_w(c, p):
                wq, wk, wvv, wo = Ws[p]
                sc.add("sp", lambda e: e.dma_start(out=wq, in_=winv[:, :, c * 128:(c + 1) * 128]),
                       r=ck_in, w=[("wq", p)], dma=f"wq{p}")
                sc.add("sp", lambda e: e.dma_start(out=wk, in_=winv[:, :, D + c * 128:D + (c + 1) * 128]),
                       r=ck_in, w=[("wk", p)], dma=f"wk{p}")
                sc.add("sp", lambda e: e.dma_start(out=wvv, in_=winv[:, :, 2 * D + c * 128:2 * D + (c + 1) * 128]),
                       r=ck_in, w=[("wvv", p)], dma=f"wvv{p}")

            def load_wo(c, p):
                wo = Ws[p][3]
                sc.add("sp", lambda e: e.dma_start(out=wo, in_=b_aout[j, c * 128:(c + 1) * 128, :]),
                       r=ck_out, w=[("wo_a", p)], dma=f"wo_a{p}")

            def proj_steps(c, p, evac_act, fixed=None):
                wq, wk, wvv, wo = Ws[p]
                qT, kT, vv = QK[p]
                steps = []
                for (wt, dst, kw, kd, scl) in ((wq, qT, ("wq", p), "qT", 0.125), (wk, kT, ("wk", p), "kT", 1.0)):
                    for tt in range(NT):
                        box = {}

                        def s1(wt=wt, tt=tt, kw=kw, box=box):
                            box["bk"] = bk = nbank(fixed)

                            def f(e):
                                for cc in range(4):
                                    ins = e.matmul(bank(bk), lhsT=wt[:, cc, :], rhs=hb[:, cc, tt * 512:(tt + 1) * 512],
                                                   start=(cc == 0), stop=False)
                                return ins
                            sc.add("pe", f, r=[kw] + hbk, w=PK(bk))

                        def s2(wt=wt, tt=tt, kw=kw, box=box, dst=dst, kd=kd, scl=scl):
                            bk = box["bk"]

                            def f(e):
                                for cc in range(4, NCH):
                                    ins = e.matmul(bank(bk), lhsT=wt[:, cc, :], rhs=hb[:, cc, tt * 512:(tt + 1) * 512],
                                                   start=False, stop=(cc == NCH - 1))
                                return ins
                            sc.add("pe", f, r=[kw] + hbk, w=PK(bk))
                            o = dst[:, tt * 512:(tt + 1) * 512]
                            if evac_act:
                                sc.add("act", lambda e: e.activation(out=o, in_=bank(bk), func=AF.Identity, scale=scl),
                                       r=PK(bk), w=[(kd, p, tt)])
                            else:
                                sc.add("dve", lambda e: e.tensor_scalar(out=o, in0=bank(bk), scalar1=scl, scalar2=None,
                                                                        op0=ALU.mult), r=PK(bk), w=[(kd, p, tt)])
                        steps += [s1, s2]
                for n4 in range(NTB // 4):
                    box = {}
                    for q in range(4):
                        def sv(n4=n4, q=q, box=box):
                            if q == 0:
                                box["bk"] = nbank(fixed)
                            bk = box["bk"]
                            n = n4 * 4 + q

                            def f(e):
                                for cc in range(NCH):
                                    ins = e.matmul(bank(bk)[:, q * 128:(q + 1) * 128],
                                                   lhsT=hb[:, cc, n * 128:(n + 1) * 128], rhs=wvv[:, cc, :],
                                                   start=(cc == 0), stop=(cc == NCH - 1))
                                return ins
                            sc.add("pe", f, r=[("wvv", p)] + hbk, w=PK(bk))
                            if q == 3:
                                sc.add("dve", lambda e: e.tensor_copy(
                                    out=vv[:, n4 * 4:(n4 + 1) * 4, :], in_=bank(bk).rearrange("p (q d) -> p q d", q=4)),
                                    r=PK(bk), w=[("vv", p, n4)])
                        steps.append(sv)
                return steps

            def outproj_steps(c, p, fixed=None):
                wo = Ws[p][3]
                oT = oTs[p]
                steps = []
                for fo in range(NCH):
                    for tt in range(NT):
                        def so(fo=fo, tt=tt):
                            bk = nbank(fixed)
                            sc.add("pe", lambda e: e.matmul(
                                bank(bk), lhsT=wo[:, fo * 128:(fo + 1) * 128], rhs=oT[:, tt * 512:(tt + 1) * 512],
                                start=True, stop=True), r=[("wo_a", p), ("oT", p, tt)], w=PK(bk))
                            resid_add(bank(bk), PK(bk), fo, tt * 512, 512)
                        steps.append(so)
                return steps

            def attention(c, p, pending):
                qT, kT, vv = QK[p]
                oT = oTs[p]
                items = []
                for T in range(NT):
                    blocks = list(range(4 * T + 3, -1, -1))
                    for i, kb in enumerate(blocks):
                        jj = kb - 4 * T
                        items.append(dict(T=T, i=i, kb=kb, c0=max(jj, 0) * 128, dg=(jj >= 0), first=(i == 0),
                                          last=(i == len(blocks) - 1)))
                G = len(items)
                per = -(-len(pending) // G) if pending else 0

                def emit_z(g):
                    it = items[g]

                    def zf(e, kb=it["kb"], c0=it["c0"], T=it["T"]):
                        for h in range(2):
                            ins = e.matmul(zv[:, h, c0:512], lhsT=kT[64 * h:64 * h + 64, kb * 128:(kb + 1) * 128],
                                           rhs=qT[64 * h:64 * h + 64, T * 512 + c0:(T + 1) * 512],
                                           start=True, stop=True)
                        return ins
                    sc.add("pe", zf, r=[("kT", p, it["kb"] // 4), ("qT", p, it["T"])], w=PK(0, 2))

                def emit_tail(g):
                    it = items[g]; c0 = it["c0"]; b = g % 2; b3 = g % 3
                    sc.add("act", lambda e: e.activation(out=gb_[:, :, c0:512], in_=cv[:, :, c0:512], func=AF.Exp),
                           r=PK(2, 2), w=["g"])
                    sc.add("dve", lambda e: e.tensor_tensor(out=abuf[b][:, :, c0:512], in0=eb[b3][:, :, c0:512],
                                                            in1=gb_[:, :, c0:512], op=ALU.mult),
                           r=["g", ("e", b3)], w=[("a", b)])

                def emit_av(g):
                    it = items[g]; c0 = it["c0"]; b = g % 2; kb = it["kb"]; first = it["first"]; T = it["T"]

                    def av(e):
                        for h in range(2):
                            ins = e.matmul(bank(4 + h)[:, c0:512], lhsT=vv[:, kb, :], rhs=abuf[b][:, h, c0:512],
                                           start=first, stop=False, skip_group_check=True)
                        return ins
                    sc.add("pe", av, r=[("a", b), ("vv", p, kb // 4)], w=PK(4, 2))
                    if it["last"]:
                        for h in range(2):
                            sc.add("dve", lambda e, h=h: e.tensor_copy(
                                out=oT[64 * h:64 * h + 64, T * 512:(T + 1) * 512], in_=bank(4 + h)[64 * h:64 * h + 64, :]),
                                r=PK(4 + h), w=[("oT", p, T)])

                def emit_fill():
                    def fill(e):
                        for _ in range(NFILL):
                            ins = e.matmul(bank(6), lhsT=negones, rhs=cst_b[:, 0:512], start=True, stop=True)
                        return ins
                    if NFILL > 0 and not pending:
                        sc.add("pe", fill, r=["cst_b"], w=PK(6))

                emit_z(0)
                for g in range(G):
                    it = items[g]; c0 = it["c0"]; b = g % 2; b3 = g % 3
                    if it["first"]:
                        sc.add("pool", lambda e: e.memset(R, 0.0), w=["R"])
                    sc.add("act", lambda e, c0=c0, b3=b3: e.activation(out=eb[b3][:, :, c0:512], in_=zv[:, :, c0:512],
                                                                       func=AF.Exp), r=PK(0, 2), w=[("e", b3)])
                    if it["dg"]:
                        mbc = mask_su.unsqueeze(1).to_broadcast([128, 2, 128])
                        sc.add("pool", lambda e, c0=c0, b3=b3, mbc=mbc: e.tensor_tensor(
                            out=eb[b3][:, :, c0:c0 + 128], in0=eb[b3][:, :, c0:c0 + 128], in1=mbc, op=ALU.mult),
                            r=[("e", b3), "cst_b"], w=[("e", b3)])
                    sc.add("act", lambda e, c0=c0, b=b, b3=b3: e.activation(out=spb[b][:, :, c0:512],
                                                                           in_=eb[b3][:, :, c0:512],
                                                                           func=AF.Ln, bias=1.0),
                           r=[("e", b3)], w=[("sp", b)])
                    if g + 1 < G:
                        emit_z(g + 1)
                    if g > 0:
                        emit_tail(g - 1)

                    def negc(e, c0=c0, b=b, first=it["first"]):
                        for h in range(2):
                            ins = e.matmul(cv[:, h, c0:512], lhsT=negtri, rhs=spb[b][:, h, c0:512], start=True,
                                           stop=first)
                            if not first:
                                ins = e.matmul(cv[:, h, c0:512], lhsT=negones, rhs=R[:, h, c0:512], start=False,
                                               stop=True)
                        return ins
                    sc.add("pe", negc, r=[("sp", b), "R", "cst_b"], w=PK(2, 2))
                    sc.add("dve", lambda e, c0=c0, b=b: e.tensor_tensor(out=R[:, :, c0:512], in0=R[:, :, c0:512],
                                                                       in1=spb[b][:, :, c0:512], op=ALU.add),
                           r=[("sp", b), "R"], w=["R"])
                    if g > 0:
                        emit_av(g - 1)
                    for _ in range(per):
                        if pending:
                            pending.pop(0)()
                    emit_fill()
                emit_tail(G - 1)
                emit_av(G - 1)
                while pending:
                    pending.pop(0)()

            if "noattn" not in dbg:
                load_w(0, 0); load_wo(0, 0)
                for stp in proj_steps(0, 0, True):
                    stp()
                for c in range(NCH):
                    p = c % 2
                    pending = []
                    if c + 1 < NCH:
                        load_w(c + 1, 1 - p)
                        pending += proj_steps(c + 1, 1 - p, False, fixed=6)
                    if c > 0:
                        opend = outproj_steps(c - 1, 1 - p, fixed=7)
                        mix = []
                        while pending or opend:
                            if pending: mix.append(pending.pop(0))
                            if pending: mix.append(pending.pop(0))
                            if opend: mix.append(opend.pop(0))
                        pending = mix
                    drain(2)
                    attention(c, p, pending)
                    if c + 1 < NCH:
                        load_wo(c + 1, 1 - p)
                for stp in outproj_steps(NCH - 1, (NCH - 1) % 2):
                    stp()
            sc.barrier()
            for T in range(NT):
                layer_norm(l, 0, T * 512, xbuf, xsqbuf, ([("lnxb", 0), ("lnxb", 1)], [("lnxsq", 0), ("lnxsq", 1)]),
                           (mean_sb, var_sb, rstd_sb))
            sc.barrier()

        for s in range(nseq):
            load_seq(s)
            for l in range(depth):
                if s == 0:
                    flush_until(2 * l)
                    cast_phase(2 * l + 2)
                if l % 2 == 0:
                    attn_layer(l)
                else:
                    gmlp_layer(l)
                if s == 0:
                    flush_until(2 * l + 1)
                    cast_phase(2 * l + 3)
                if "noffn" not in dbg:
                    ffn_layer(l)
            store_seq(s)
        sc.add("sp", None, r=[("yout", s, n) for s in range(nseq) for n in range(NTB)])
        sc.emit(nc, st)
    return nc, sc


def host_prep(inputs, depth):
    NG = depth // 2
    f = np.float32
    lnp = np.zeros((128, depth * 32), f)
    for l in range(depth):
        for which, (g, b) in enumerate(((inputs["ln_mix_g"], inputs["ln_mix_b"]), (inputs["ln_ffn_g"], inputs["ln_ffn_b"]))):
            lnp[:, l * 32 + which * 16: l * 32 + which * 16 + 8] = np.asarray(g[l], f).reshape(8, 128).T
            lnp[:, l * 32 + which * 16 + 8: l * 32 + which * 16 + 16] = np.asarray(b[l], f).reshape(8, 128).T
    convp = np.zeros((128, depth * 176), f)
    for l in range(depth):
        for k in range(3):
            convp[:, l * 176 + k * 44: l * 176 + (k + 1) * 44] = np.asarray(inputs["ffn_conv_w"][l, k], f).reshape(44, 128).T
        convp[:, l * 176 + 132: l * 176 + 176] = np.asarray(inputs["ffn_conv_b"][l], f).reshape(44, 128).T
    idx = np.arange(128)
    consts = np.zeros((128, 768), f)
    consts[:, 0:128] = np.eye(128, dtype=f)
    consts[:, 128:256] = -(idx[:, None] >= idx[None, :]).astype(f)
    consts[:, 256:384] = -1.0
    consts[:, 384:512] = (idx[:, None] < idx[None, :]).astype(f)
    consts[:, 512:640] = (idx[:, None] <= idx[None, :]).astype(f)
    consts[:, 640:768] = 1.0 / 1024.0
    ng = max(NG, 1)
    gb = np.zeros((ng, 2, 128, E2), f)
    bs = np.zeros((ng, 1, GG * 128), f)
    for j in range(NG):
        gb[j, 0] = np.broadcast_to(np.asarray(inputs["gmlp_ln_g"][j], f)[None, :], (128, E2))
        gb[j, 1] = np.broadcast_to(np.asarray(inputs["gmlp_ln_b"][j], f)[None, :], (128, E2))
        bs[j, 0] = np.asarray(inputs["gmlp_b_s"][j], f).reshape(-1)
    return {"lnp": lnp, "convp": convp, "consts": consts, "gb": gb, "bs": bs}


_CACHE = {}


def run(inputs, ncores, nseq, S, depth, ffn_t=1024, trace=False, dbg=()):
    key = (nseq, S, depth, ffn_t)
    if key not in _CACHE:
        _CACHE[key] = build(nseq, S, depth, ffn_t, dbg)
    nc, sc = _CACHE[key]
    NA = (depth + 1) // 2
    NG = max(depth // 2, 1)
    f = np.float32
    small = host_prep(inputs, depth)
    shared = {
        "attn_w_in": np.ascontiguousarray(inputs["attn_w_in"][:NA], f),
        "attn_w_out": np.ascontiguousarray(inputs["attn_w_out"][:NA], f),
        "gmlp_w_in": np.ascontiguousarray(inputs["gmlp_w_in"][:NG], f),
        "gmlp_w_out": np.ascontiguousarray(inputs["gmlp_w_out"][:NG], f),
        "ffn_w_up": np.ascontiguousarray(inputs["ffn_w_up"][:depth], f),
        "ffn_w_down": np.ascontiguousarray(inputs["ffn_w_down"][:depth], f),
        "gmlp_w_s": np.ascontiguousarray(inputs["gmlp_w_s"][:NG], f),
    }
    shared.update(small)
    x = np.asarray(inputs["x"], f)
    in_maps = []
    for i in range(ncores):
        m = dict(shared)
        m["x"] = np.ascontiguousarray(x[i * nseq:(i + 1) * nseq, :S])
        in_maps.append(m)
    res = run_bass_kernel_spmd(nc, in_maps, core_ids=list(range(ncores)), trace=trace)
    out = np.concatenate([np.asarray(r["y"]) for r in res.results], axis=0)
    return out.astype(np.float32), res


def kernel(**inputs):
    out, _ = run(inputs, 8, 4, 2048, 4)
    return out
```
